# Optimizing a Trainium2 kernel written in Bass

```python
import jax
import jax.numpy as jnp
from jax import lax
import numpy as np

D_MODEL = 1024
BATCH = 32
SEQ = 2048
DEPTH = 2

GRID_W = 64
CTX_LEN = 256
N_MIXERS = 2
N_ATTN_LAYERS = (DEPTH + N_MIXERS - 1) // N_MIXERS
N_RET_LAYERS = DEPTH // N_MIXERS

HEAD_DIM = 64
N_HEADS = D_MODEL // HEAD_DIM
N_KV_HEADS = N_HEADS // 4
GQA_GROUP = N_HEADS // N_KV_HEADS
WINDOW = 128
ATTN_BLOCK = 128
ATTN_PROJ = (N_HEADS + 2 * N_KV_HEADS) * HEAD_DIM

RET_HEADS = D_MODEL // 256
RET_QK_DIM = D_MODEL // RET_HEADS
RET_V_DIM = 2 * D_MODEL // RET_HEADS
RET_VWIDTH = 2 * D_MODEL
RET_CHUNK = 128
RET_PROJ = 2 * D_MODEL + 2 * RET_VWIDTH

D_FF = -(-8 * D_MODEL // (3 * 256)) * 256

ROPE_BASE = 10000.0
EPS = 1e-6
NEG_INF = -1e30

kernel_name = 'hybrid_swa_sink_retention_dit'


def rms_norm(x, g):
    xf = x.astype(jnp.float32)
    y = xf * lax.rsqrt(jnp.mean(xf * xf, axis=-1, keepdims=True) + EPS)
    return (y * g.astype(jnp.float32)).astype(x.dtype)


def modulate(h, shift, scale):
    return h * (1 + scale) + shift


def grid_positions(n):
    rows = n // GRID_W
    row = jnp.broadcast_to(jnp.arange(rows, dtype=jnp.int32)[:, None], (rows, GRID_W)).reshape(n)
    col = jnp.broadcast_to(jnp.arange(GRID_W, dtype=jnp.int32)[None, :], (rows, GRID_W)).reshape(n)
    return row, col


def rope_tables(n, head_dim):
    row, col = grid_positions(n)
    axis_dim = head_dim // 2
    inv = ROPE_BASE ** (-jnp.arange(0, axis_dim, 2, dtype=jnp.float32) / axis_dim)
    ang_r = row.astype(jnp.float32)[:, None] * inv
    ang_c = col.astype(jnp.float32)[:, None] * inv
    return jnp.cos(ang_r), jnp.sin(ang_r), jnp.cos(ang_c), jnp.sin(ang_c)


def rotate_axis(x, cos, sin):
    x1, x2 = jnp.split(x, 2, axis=-1)
    cos = cos[:, None, :]
    sin = sin[:, None, :]
    return jnp.concatenate([x1 * cos - x2 * sin, x1 * sin + x2 * cos], axis=-1)


def rope_2d(x, tables):
    cos_r, sin_r, cos_c, sin_c = tables
    xr, xc = jnp.split(x.astype(jnp.float32), 2, axis=-1)
    out = jnp.concatenate([rotate_axis(xr, cos_r, sin_r), rotate_axis(xc, cos_c, sin_c)], axis=-1)
    return out.astype(x.dtype)


def swiglu(h, w_in, w_out):
    gate, up = jnp.split(h @ w_in, 2, axis=-1)
    return (jax.nn.silu(gate) * up) @ w_out


def windowed_gqa_sink(h_x, h_c, w_qkv, q_gain, k_gain, sink, w_o, need_ctx_out):
    B, S, _ = h_x.shape
    L = h_c.shape[1]
    nb = S // ATTN_BLOCK
    band = ATTN_BLOCK + 2 * WINDOW
    scale = HEAD_DIM ** -0.5
    qd = N_HEADS * HEAD_DIM
    kvd = N_KV_HEADS * HEAD_DIM
    tables = rope_tables(S, HEAD_DIM)
    sink_g = sink.astype(jnp.float32).reshape(N_KV_HEADS, GQA_GROUP)[None, :, :, None, None]

    q_x, k_x, v_x = jnp.split(h_x @ w_qkv, [qd, qd + kvd], axis=-1)
    q_x = rope_2d(rms_norm(q_x.reshape(B, S, N_HEADS, HEAD_DIM), q_gain), tables)
    q_x = q_x.reshape(B, S, N_KV_HEADS, GQA_GROUP, HEAD_DIM)
    k_x = rope_2d(rms_norm(k_x.reshape(B, S, N_KV_HEADS, HEAD_DIM), k_gain), tables)
    v_x = v_x.reshape(B, S, N_KV_HEADS, HEAD_DIM)
    k_c, v_c = jnp.split(h_c @ w_qkv[:, qd:], 2, axis=-1)
    k_c = rms_norm(k_c.reshape(B, L, N_KV_HEADS, HEAD_DIM), k_gain)
    v_c = v_c.reshape(B, L, N_KV_HEADS, HEAD_DIM)

    pad = ((0, 0), (WINDOW, WINDOW), (0, 0), (0, 0))
    k_pad = jnp.pad(k_x, pad)
    v_pad = jnp.pad(v_x, pad)
    r_idx = jnp.arange(ATTN_BLOCK, dtype=jnp.int32)[:, None]
    n_idx = jnp.arange(band, dtype=jnp.int32)[None, :]
    in_window = (n_idx >= r_idx) & (n_idx - r_idx <= 2 * WINDOW)

    def block(b):
        start = b * ATTN_BLOCK
        qb = lax.dynamic_slice_in_dim(q_x, start, ATTN_BLOCK, axis=1)
        kb = lax.dynamic_slice_in_dim(k_pad, start, band, axis=1)
        vb = lax.dynamic_slice_in_dim(v_pad, start, band, axis=1)
        key_pos = start - WINDOW + n_idx
        valid = in_window & (key_pos >= 0) & (key_pos < S)
        s_ctx = jnp.einsum('bqkgd,bnkd->bkgqn', qb, k_c, preferred_element_type=jnp.float32) * scale
        s_loc = jnp.einsum('bqkgd,bnkd->bkgqn', qb, kb, preferred_element_type=jnp.float32) * scale
        s_loc = jnp.where(valid, s_loc, NEG_INF)
        sink_col = jnp.broadcast_to(sink_g, s_ctx.shape[:-1] + (1,))
        p = jax.nn.softmax(jnp.concatenate([s_ctx, s_loc, sink_col], axis=-1), axis=-1).astype(vb.dtype)
        return (jnp.einsum('bkgqn,bnkd->bqkgd', p[..., :L], v_c)
                + jnp.einsum('bkgqn,bnkd->bqkgd', p[..., L:L + band], vb))

    o_x = jnp.moveaxis(lax.map(block, jnp.arange(nb, dtype=jnp.int32)), 0, 1).reshape(B, S, qd)
    out_x = o_x @ w_o

    out_c = None
    if need_ctx_out:
        q_c = rms_norm((h_c @ w_qkv[:, :qd]).reshape(B, L, N_HEADS, HEAD_DIM), q_gain)
        q_c = q_c.reshape(B, L, N_KV_HEADS, GQA_GROUP, HEAD_DIM)
        s_c = jnp.einsum('bqkgd,bnkd->bkgqn', q_c, k_c, preferred_element_type=jnp.float32) * scale
        sink_col = jnp.broadcast_to(sink_g, s_c.shape[:-1] + (1,))
        p_c = jax.nn.softmax(jnp.concatenate([s_c, sink_col], axis=-1), axis=-1).astype(v_c.dtype)
        o_c = jnp.einsum('bkgqn,bnkd->bqkgd', p_c[..., :L], v_c)
        out_c = o_c.reshape(B, L, qd) @ w_o
    return out_c, out_x


def retention_chunked(q, k, v, log_gamma, state0):
    B, T, H, _ = q.shape
    nc = T // RET_CHUNK
    pos = jnp.arange(RET_CHUNK, dtype=jnp.float32)
    diff = pos[:, None] - pos[None, :]
    intra = jnp.where(diff >= 0, jnp.exp(log_gamma[:, None, None] * jnp.maximum(diff, 0.0)), 0.0)
    q_decay = jnp.exp(log_gamma[None, :] * (pos + 1.0)[:, None])[None, :, :, None]
    k_decay = jnp.exp(log_gamma[None, :] * (RET_CHUNK - 1.0 - pos)[:, None])[None, :, :, None]
    chunk_decay = jnp.exp(log_gamma * RET_CHUNK)[None, :, None, None]

    def to_chunks(a):
        return jnp.moveaxis(a.reshape(B, nc, RET_CHUNK, H, a.shape[-1]), 1, 0)

    def step(state, inp):
        qc, kc, vc = inp
        scores = jnp.einsum('bnhd,bmhd->bhnm', qc, kc) * intra
        inner = jnp.einsum('bhnm,bmhe->bnhe', scores, vc)
        cross = jnp.einsum('bnhd,bhde->bnhe', qc, state) * q_decay
        new_state = state * chunk_decay + jnp.einsum('bmhd,bmhe->bhde', kc * k_decay, vc)
        return new_state, inner + cross

    state, out = lax.scan(step, state0, (to_chunks(q), to_chunks(k), to_chunks(v)))
    return jnp.moveaxis(out, 0, 1).reshape(B, T, H, v.shape[-1]), state


def retention_final_state(k, v, log_gamma):
    T = k.shape[1]
    pos = jnp.arange(T, dtype=jnp.float32)
    decay = jnp.exp(log_gamma[None, :] * (T - 1.0 - pos)[:, None])[None, :, :, None]
    return jnp.einsum('bthd,bthe->bhde', k * decay, v)


def bidir_retention(h_x, h_c, w_qkvg, decay_logit, gn_gain, w_o, need_ctx_out):
    B, S, _ = h_x.shape
    L = h_c.shape[1]
    qk = RET_HEADS * RET_QK_DIM
    f32 = jnp.float32
    tables = rope_tables(S, RET_QK_DIM)
    log_g = jax.nn.log_sigmoid(decay_logit.astype(f32))
    k_scale = RET_QK_DIM ** -0.5

    q_x, k_x, v_x, g_x = jnp.split(h_x @ w_qkvg, [qk, 2 * qk, 2 * qk + RET_VWIDTH], axis=-1)
    q_x = rope_2d(q_x.reshape(B, S, RET_HEADS, RET_QK_DIM), tables).astype(f32)
    k_x = rope_2d(k_x.reshape(B, S, RET_HEADS, RET_QK_DIM), tables).astype(f32) * k_scale
    v_x = v_x.reshape(B, S, RET_HEADS, RET_V_DIM).astype(f32)
    k_c, v_c = jnp.split(h_c @ w_qkvg[:, qk:2 * qk + RET_VWIDTH], [qk], axis=-1)
    k_c = k_c.reshape(B, L, RET_HEADS, RET_QK_DIM).astype(f32) * k_scale
    v_c = v_c.reshape(B, L, RET_HEADS, RET_V_DIM).astype(f32)

    def flip(a):
        return jnp.flip(a, axis=1)

    def gated_out(o, g):
        mu = jnp.mean(o, axis=-1, keepdims=True)
        var = jnp.mean(jnp.square(o - mu), axis=-1, keepdims=True)
        y = ((o - mu) * lax.rsqrt(var + EPS)).reshape(o.shape[0], o.shape[1], RET_VWIDTH) * gn_gain.astype(f32)
        return (jax.nn.silu(g) * y.astype(g.dtype)) @ w_o

    out_c = None
    if need_ctx_out:
        q_c = (h_c @ w_qkvg[:, :qk]).reshape(B, L, RET_HEADS, RET_QK_DIM).astype(f32)
        g_c = h_c @ w_qkvg[:, 2 * qk + RET_VWIDTH:]
        zero_state = jnp.zeros((B, RET_HEADS, RET_QK_DIM, RET_V_DIM), f32)
        oc_f, state_f = retention_chunked(q_c, k_c, v_c, log_g[0], zero_state)
        oc_b, state_b = retention_chunked(flip(q_c), flip(k_c), flip(v_c), log_g[1], zero_state)
        out_c = gated_out(oc_f + flip(oc_b), g_c)
    else:
        state_f = retention_final_state(k_c, v_c, log_g[0])
        state_b = retention_final_state(flip(k_c), flip(v_c), log_g[1])

    ox_f, _ = retention_chunked(q_x, k_x, v_x, log_g[0], state_f)
    ox_b, _ = retention_chunked(flip(q_x), flip(k_x), flip(v_x), log_g[1], state_b)
    out_x = gated_out(ox_f + flip(ox_b), g_x)
    return out_c, out_x


def setup_inputs(seed: int = 0) -> dict:
    key = jax.random.key(seed)
    ks = jax.random.split(key, 19)
    f32 = jnp.float32

    def nrm(k, shape, scale):
        return jax.random.normal(k, shape, f32) * scale

    gamma = 1.0 - 2.0 ** (-5.0 - np.arange(RET_HEADS))
    decay_init = jnp.asarray(np.log(gamma / (1.0 - gamma)), dtype=f32)
    return {
        'x': nrm(ks[0], (BATCH, SEQ, D_MODEL), 1.0),
        'c': nrm(ks[1], (BATCH, D_MODEL), 1.0),
        'ctx': nrm(ks[2], (BATCH, CTX_LEN, D_MODEL), 1.0),
        'c_ctx': nrm(ks[3], (D_MODEL,), 1.0),
        'ada_w': nrm(ks[4], (DEPTH, D_MODEL, 6 * D_MODEL), 0.5 * D_MODEL ** -0.5),
        'ada_b': nrm(ks[5], (DEPTH, 6 * D_MODEL), 0.02),
        'norm1_g': 1.0 + nrm(ks[6], (DEPTH, D_MODEL), 0.02),
        'norm2_g': 1.0 + nrm(ks[7], (DEPTH, D_MODEL), 0.02),
        'ffn_w_in': nrm(ks[8], (DEPTH, D_MODEL, 2 * D_FF), D_MODEL ** -0.5),
        'ffn_w_out': nrm(ks[9], (DEPTH, D_FF, D_MODEL), D_FF ** -0.5),
        'attn_w_qkv': nrm(ks[10], (N_ATTN_LAYERS, D_MODEL, ATTN_PROJ), D_MODEL ** -0.5),
        'attn_q_norm': 1.0 + nrm(ks[11], (N_ATTN_LAYERS, HEAD_DIM), 0.02),
        'attn_k_norm': 1.0 + nrm(ks[12], (N_ATTN_LAYERS, HEAD_DIM), 0.02),
        'attn_sink': nrm(ks[13], (N_ATTN_LAYERS, N_HEADS), 0.5),
        'attn_w_o': nrm(ks[14], (N_ATTN_LAYERS, N_HEADS * HEAD_DIM, D_MODEL), (N_HEADS * HEAD_DIM) ** -0.5),
        'ret_w_qkvg': nrm(ks[15], (N_RET_LAYERS, D_MODEL, RET_PROJ), D_MODEL ** -0.5),
        'ret_decay_logit': decay_init[None, None, :] + nrm(ks[16], (N_RET_LAYERS, 2, RET_HEADS), 0.01),
        'ret_gn_g': 1.0 + nrm(ks[17], (N_RET_LAYERS, RET_VWIDTH), 0.02),
        'ret_w_o': nrm(ks[18], (N_RET_LAYERS, RET_VWIDTH, D_MODEL), RET_VWIDTH ** -0.5),
    }


def reference(x, c, ctx, c_ctx, ada_w, ada_b, norm1_g, norm2_g, ffn_w_in, ffn_w_out,
              attn_w_qkv, attn_q_norm, attn_k_norm, attn_sink, attn_w_o,
              ret_w_qkvg, ret_decay_logit, ret_gn_g, ret_w_o):
    c_act = jax.nn.silu(c)[:, None, :]
    cc_act = jax.nn.silu(c_ctx)[None, None, :]
    y_ctx = ctx
    for i in range(DEPTH):
        need_ctx_out = i < DEPTH - 1
        mx = jnp.split(c_act @ ada_w[i] + ada_b[i], 6, axis=-1)
        mc = jnp.split(cc_act @ ada_w[i] + ada_b[i], 6, axis=-1)
        h_x = modulate(rms_norm(x, norm1_g[i]), mx[0], mx[1])
        h_c = modulate(rms_norm(y_ctx, norm1_g[i]), mc[0], mc[1])
        j = i // N_MIXERS
        if i % N_MIXERS == 0:
            out_c, out_x = windowed_gqa_sink(h_x, h_c, attn_w_qkv[j], attn_q_norm[j], attn_k_norm[j],
                                             attn_sink[j], attn_w_o[j], need_ctx_out)
        else:
            out_c, out_x = bidir_retention(h_x, h_c, ret_w_qkvg[j], ret_decay_logit[j], ret_gn_g[j],
                                           ret_w_o[j], need_ctx_out)
        x = x + mx[2] * out_x
        x = x + mx[5] * swiglu(modulate(rms_norm(x, norm2_g[i]), mx[3], mx[4]), ffn_w_in[i], ffn_w_out[i])
        if need_ctx_out:
            y_ctx = y_ctx + mc[2] * out_c
            y_ctx = y_ctx + mc[5] * swiglu(modulate(rms_norm(y_ctx, norm2_g[i]), mc[3], mc[4]),
                                           ffn_w_in[i], ffn_w_out[i])
    return x
```

```python
import contextlib
import numpy as np
import concourse.bass as bass
import concourse.mybir as mybir
from concourse.bass_utils import run_bass_kernel_spmd

F32 = mybir.dt.float32
BF16 = mybir.dt.bfloat16
ALU = mybir.AluOpType
AF = mybir.ActivationFunctionType
AX = mybir.AxisListType

SEM_LIMIT = 30000
N_DMA_SEMS = 6
D = 1024
S = 2048
LCTX = 256
NT = 18
DFF = 2816
NJ = 22
EPS = 1e-6


class Buf:
    __slots__ = ("name", "lw", "rd")

    def __init__(self, name=""):
        self.name = name
        self.lw = None
        self.rd = {}


class Prog:
    ENGS = ("pe", "act", "dve", "pool", "sp")

    def __init__(self, nc):
        self.nc = nc
        self.stack = contextlib.ExitStack()
        self.ops = {e: [] for e in self.ENGS}
        self.cnt = {e: 0 for e in self.ENGS}
        self.sem = {}
        self.waited = {e: {} for e in self.ENGS}
        self.pending = {e: {} for e in self.ENGS}
        self.last_ev = {e: None for e in self.ENGS}
        self.dma_out = []
        self.nsem = 0
        for e in self.ENGS:
            self.sem[e] = self._new_sem(e)
        self.dma_sems = {}
        self.dma_cnt = {}
        self.dma_rr = {}
        for q in ("sp", "pool", "act"):
            self.dma_sems[q] = [self._new_sem("d" + q) for _ in range(N_DMA_SEMS)]
            self.dma_cnt[q] = [0] * N_DMA_SEMS
            self.dma_rr[q] = 0
        self.n_ops = 0
        self.limit = None
        self.marks = {}
        self.phases = []

    def mark(self, name):
        self.marks.setdefault(name, self.n_ops)

    def phase(self, name):
        self.phases.append((name, len(self.ops['pe']), len(self.ops['act']), len(self.ops['dve']), len(self.ops['pool'])))

    def _new_sem(self, tag):
        self.nsem += 1
        return self.stack.enter_context(self.nc.semaphore(f"s_{tag}_{self.nsem}"))

    def sbuf(self, name, shape, dt):
        return self.stack.enter_context(self.nc.sbuf_tensor("sb_" + name, list(shape), dt))

    def psum(self, name, shape, dt):
        return self.stack.enter_context(self.nc.psum_tensor(name, list(shape), dt))

    def op(self, eng, fn, reads=(), writes=(), dma=False):
        if self.limit is not None and self.n_ops >= self.limit:
            return None
        waits = self.pending[eng]
        self.pending[eng] = {}
        wd = self.waited[eng]
        for s in list(waits.keys()):
            if wd.get(s, 0) >= waits[s]:
                del waits[s]

        def need(ev):
            if ev is None:
                return
            sem, val = ev
            if eng == "pe" and not dma and sem is self.sem["pe"]:
                return
            if wd.get(sem, 0) >= val:
                return
            if waits.get(sem, 0) < val:
                waits[sem] = val

        for b in reads:
            need(b.lw)
        for b in writes:
            need(b.lw)
            for s, v in b.rd.items():
                need((s, v))
        if dma:
            i = self.dma_rr[eng]
            self.dma_rr[eng] = (i + 1) % N_DMA_SEMS
            if self.dma_cnt[eng][i] * 16 + 16 > SEM_LIMIT:
                self.dma_sems[eng][i] = self._new_sem("d" + eng)
                self.dma_cnt[eng][i] = 0
            sem = self.dma_sems[eng][i]
            need((sem, self.dma_cnt[eng][i] * 16))
            self.dma_cnt[eng][i] += 1
            ev = (sem, self.dma_cnt[eng][i] * 16)
            inc = 16
            self.dma_out.append(ev)
        else:
            if self.cnt[eng] + 1 > SEM_LIMIT:
                self.sem[eng] = self._new_sem(eng)
                self.cnt[eng] = 0
            self.cnt[eng] += 1
            ev = (self.sem[eng], self.cnt[eng])
            inc = 1
            self.last_ev[eng] = ev
        for s, v in waits.items():
            wd[s] = v
        self.ops[eng].append((list(waits.items()), fn, ev[0], inc))
        for b in reads:
            if b.rd.get(ev[0], 0) < ev[1]:
                b.rd[ev[0]] = ev[1]
        for b in writes:
            b.lw = ev
            b.rd = {}
        self.n_ops += 1
        return ev

    def pe(self, fn, reads=(), writes=()):
        return self.op("pe", fn, reads, writes)

    def act(self, fn, reads=(), writes=()):
        return self.op("act", fn, reads, writes)

    def dve(self, fn, reads=(), writes=()):
        return self.op("dve", fn, reads, writes)

    def pool(self, fn, reads=(), writes=()):
        return self.op("pool", fn, reads, writes)

    def dma(self, q, out, in_, reads=(), writes=(), **kw):
        return self.op(q, lambda e: e.dma_start(out=out, in_=in_, **kw), reads, writes, dma=True)

    def barrier(self):
        evs = [ev for ev in self.last_ev.values() if ev is not None] + self.dma_out
        self.dma_out = []
        for e in self.ENGS:
            p = self.pending[e]
            for s, v in evs:
                if p.get(s, 0) < v:
                    p[s] = v

    def emit(self):
        nc = self.nc
        with nc.Block() as block:
            def replay(name):
                def run(e):
                    for waits, fn, sem, inc in self.ops[name]:
                        for s, v in waits:
                            e.wait_ge(s, v)
                        ins = fn(e)
                        ins.then_inc(sem, inc)
                    for s, v in self.pending[name].items():
                        e.wait_ge(s, v)
                return run

            block.tensor(replay("pe"))
            block.scalar(replay("act"))
            block.vector(replay("dve"))
            block.gpsimd(replay("pool"))
            block.sync(replay("sp"))
        self.stack.close()


def make_consts():
    c = {}
    p = np.arange(128)
    inv = 10000.0 ** (-np.arange(0, 32, 2, dtype=np.float32) / 32.0)
    acos = np.ones((128, 17, 64), np.float32)
    asin = np.zeros((128, 17, 64), np.float32)
    for tl in range(16):
        pos = tl * 128 + p
        row = (pos // 64).astype(np.float32)
        col = (pos % 64).astype(np.float32)
        ar = row[:, None] * inv[None, :]
        ac = col[:, None] * inv[None, :]
        acos[:, tl, :] = np.concatenate([np.cos(ar), np.cos(ar), np.cos(ac), np.cos(ac)], axis=1)
        asin[:, tl, :] = np.concatenate([-np.sin(ar), np.sin(ar), -np.sin(ac), np.sin(ac)], axis=1)
    c["acos"] = acos
    c["asin"] = asin
    inv2 = 10000.0 ** (-np.arange(0, 128, 2, dtype=np.float32) / 128.0)
    rcos = np.ones((128, 17, 64), np.float32)
    rsin = np.zeros((128, 17, 64), np.float32)
    for tl in range(16):
        pos = tl * 128 + p
        row = (pos // 64).astype(np.float32)
        ar = row[:, None] * inv2[None, :]
        rcos[:, tl, :] = np.cos(ar)
        rsin[:, tl, :] = np.sin(ar)
    colp = (p % 64).astype(np.float32)
    ac = colp[:, None] * inv2[None, :]
    rcos[:, 16, :] = np.cos(ac)
    rsin[:, 16, :] = np.sin(ac)
    c["rcos"] = rcos
    c["rsin"] = rsin
    m = p[:, None]
    n = p[None, :]
    misc = np.zeros((128, 10, 128), np.float32)
    misc[:, 0, :] = (m == n)
    misc[:, 1, :] = (m >= n)
    misc[:, 2, :] = (m <= n)
    misc[:, 3, :] = np.maximum(n - m, 0)
    misc[:, 4, :] = np.maximum(m - n, 0)
    misc[:, 5, :] = (n >= m)
    misc[:, 6, :] = (m >= n)
    misc[:, 7, :] = n + 1.0
    misc[:, 8, :] = 128.0 - n
    misc[:, 9, 0] = 127.0 - p
    misc[:, 9, 1] = p
    misc[:, 9, 2] = 1.0
    misc[:, 9, 3] = p + 1.0
    misc[:, 9, 4] = 128.0 - p
    c["misc"] = misc
    return c


CONST_SHAPES = {"acos": [128, 17, 64], "asin": [128, 17, 64], "rcos": [128, 17, 64],
                "rsin": [128, 17, 64], "misc": [128, 10, 128]}


def build_program(NB, n_layers=2, stage=None, limit=None):
    nc = bass.Bass("TRN2", target_bir_lowering=False)
    R = NB + 1
    dram = {}

    def din(name, shape):
        dram[name] = nc.dram_tensor(name, list(shape), F32, kind="ExternalInput").ap()
        return dram[name]

    x_d = din("x", [NB, S, D])
    ctx_d = din("ctx", [NB, LCTX, D])
    cT_d = din("cT", [128, 8, R])
    adaw_d = din("ada_w_r", [2, 12, 128, 8, 512])
    adab_d = din("ada_bT", [2, 128, 48])
    n1g_d = din("norm1_gT", [2, 128, 8])
    n2g_d = din("norm2_gT", [2, 128, 8])
    win_d = din("ffn_w_in_r", [2, NJ, 128, 2, 8, 128])
    wout_d = din("ffn_w_out_r", [2, 128, NJ, D])
    wqkv_d = din("attn_w_qkv_r", [4, 128, 8, 384])
    awo_d = din("attn_w_o_r", [4, 64, 4, D])
    again_d = din("attn_gains", [1, 320])
    asink_d = din("attn_sink", [1, 16])
    rw_d = din("ret_w_r", [4, 128, 8, 1536])
    rdl_d = din("ret_decay_logit", [1, 8])
    rgn_d = din("ret_gn_g", [1, 2048])
    rwo_d = din("ret_w_o_r", [4, 128, 4, D])
    for k, shp in CONST_SHAPES.items():
        din(k, shp)
    y_d = nc.dram_tensor("y", [NB, S, D], F32, kind="ExternalOutput").ap()
    dbg_d = nc.dram_tensor("dbg", [128, 2048], F32, kind="ExternalOutput").ap() if stage else None
    sb_d = nc.dram_tensor("sb_scratch", [16, 128, 1024], BF16, kind="ExternalOutput").ap()

    P = Prog(nc)
    P.limit = limit

    X = P.sbuf("X", [128, NT, D], F32)
    bX = [Buf(f"X{t}") for t in range(NT)]
    HT = P.sbuf("HT", [128, 8, NT * 128], BF16)
    bHT = [Buf(f"HT{t}") for t in range(NT)]
    identf = P.sbuf("identf", [128, 128], F32)
    rtab = P.sbuf("rtab", [128, 9, 128], F32)
    DT = P.sbuf("DT", [128, 4, 128], F32)
    bret = Buf("ret_tabs")
    identb = P.sbuf("identb", [128, 128], BF16)
    masks = P.sbuf("masks", [128, 2, 128], BF16)
    onesf = P.sbuf("onesf", [128, 128], F32)
    epsb = P.sbuf("epsb", [128, 1], F32)
    bconst = Buf("const")
    modT = P.sbuf("modT", [128, 2, 48, R], F32)
    bmod = Buf("modT")
    AB = P.sbuf("AB", [128, 2, 2, 2, 8, R], F32)
    bAB = Buf("AB")
    n1g = P.sbuf("n1g", [128, 2, 8], F32)
    n2g = P.sbuf("n2g", [128, 2, 8], F32)
    adab = P.sbuf("adab", [128, 2, 48], F32)
    cact = P.sbuf("cact", [128, 8, R], BF16)
    cTs = P.sbuf("cTs", [128, 8, R], F32)
    Gbc0 = P.sbuf("Gbc0", [128, D], F32)
    Gbc = [Gbc0, None]
    bG = [Buf("G0"), Buf("G1")]
    ARENA_B = 81 * 1024 + 512
    arena = P.sbuf("arena", [128, ARENA_B // 2], BF16)

    class Carver:
        def __init__(self):
            self.off = 0

        def take(self, shape, dt):
            n = int(np.prod(shape[1:]))
            nb = n * (4 if dt == F32 else 2)
            nb = (nb + 63) // 64 * 64
            assert self.off + nb <= ARENA_B, ("arena overflow", self.off + nb)
            a = arena[0:shape[0], self.off // 2:(self.off + nb) // 2]
            self.off += nb
            if dt == F32:
                a = a.bitcast(F32)
            a = a[:, 0:n]
            if len(shape) == 3:
                a = a.rearrange("p (a b) -> p a b", a=shape[1])
            elif len(shape) == 4:
                a = a.rearrange("p (a b c) -> p a b c", a=shape[1], b=shape[2])
            return a

    banks = [P.psum(f"bank{i}", [128, 512], F32) for i in range(8)]
    bbank = [Buf(f"bank{i}") for i in range(8)]
    bank_rr = [0]

    def bank():
        i = bank_rr[0]
        bank_rr[0] = (i + 1) % 8
        return banks[i], bbank[i]

    cv0 = Carver()
    misc = cv0.take([128, 10, 128], F32)
    bmisc = Buf("misc")
    P.dma("sp", misc[:], dram["misc"][:], writes=[bmisc])
    P.dve(lambda e: e.tensor_copy(out=identb[:], in_=misc[:, 0, :]), reads=[bmisc], writes=[bconst])
    P.dve(lambda e: e.tensor_copy(out=identf[:], in_=misc[:, 0, :]), reads=[bmisc], writes=[bconst])
    P.dve(lambda e: e.tensor_copy(out=masks[:], in_=misc[:, 1:3, :]), reads=[bmisc], writes=[bconst])
    dl = cv0.take([128, 8], F32)
    sc = cv0.take([128, 2, 128], F32)
    bsc = Buf("sc")
    P.dma("sp", dl[:], rdl_d.partition_broadcast(128), writes=[bret])
    lgc = rtab[:, 8, 0:8]
    kdc = rtab[:, 8, 8:16]
    cdc = rtab[:, 8, 16:24]
    P.act(lambda e: e.activation(out=dl[:], in_=dl[:], func=AF.Exp, scale=-1.0), reads=[bret], writes=[bret])
    P.act(lambda e: e.activation(out=dl[:], in_=dl[:], func=AF.Ln, scale=1.0, bias=misc[:, 9, 2:3]), reads=[bret, bmisc], writes=[bret])
    P.dve(lambda e: e.tensor_scalar(out=lgc, in0=dl[:], scalar1=-1.0, scalar2=None, op0=ALU.mult), reads=[bret], writes=[bret])
    P.act(lambda e: e.activation(out=cdc, in_=lgc, func=AF.Exp, scale=128.0), reads=[bret], writes=[bret])
    for h in range(4):
        P.act(lambda e, h=h: e.activation(out=sc[:, 0, :], in_=misc[:, 3, :], func=AF.Exp, scale=rtab[:, 8, h:h + 1]),
              reads=[bret, bmisc, bsc], writes=[bsc])
        P.act(lambda e, h=h: e.activation(out=sc[:, 1, :], in_=misc[:, 4, :], func=AF.Exp, scale=rtab[:, 8, 4 + h:5 + h]),
              reads=[bret, bmisc, bsc], writes=[bsc])
        P.dve(lambda e: e.scalar_tensor_tensor(out=sc[:, 0, :], in0=sc[:, 0, :], scalar=0.0625, in1=misc[:, 5, :],
                                               op0=ALU.mult, op1=ALU.mult), reads=[bsc, bmisc], writes=[bsc])
        P.dve(lambda e: e.scalar_tensor_tensor(out=sc[:, 1, :], in0=sc[:, 1, :], scalar=0.0625, in1=misc[:, 6, :],
                                               op0=ALU.mult, op1=ALU.mult), reads=[bsc, bmisc], writes=[bsc])
        P.dve(lambda e, h=h: e.tensor_tensor(out=DT[:, h, :], in0=sc[:, 0, :], in1=sc[:, 1, :], op=ALU.add),
              reads=[bsc], writes=[bret])
        P.act(lambda e, h=h: e.activation(out=rtab[:, 8, 24 + h:25 + h], in_=misc[:, 9, 3:4], func=AF.Exp, scale=rtab[:, 8, h:h + 1]),
              reads=[bret, bmisc], writes=[bret])
        P.act(lambda e, h=h: e.activation(out=rtab[:, 8, 28 + h:29 + h], in_=misc[:, 9, 4:5], func=AF.Exp, scale=rtab[:, 8, 4 + h:5 + h]),
              reads=[bret, bmisc], writes=[bret])
        P.act(lambda e, h=h: e.activation(out=rtab[:, 8, 8 + h:9 + h], in_=misc[:, 9, 0:1], func=AF.Exp, scale=rtab[:, 8, h:h + 1]),
              reads=[bret, bmisc], writes=[bret])
        P.act(lambda e, h=h: e.activation(out=rtab[:, 8, 12 + h:13 + h], in_=misc[:, 9, 1:2], func=AF.Exp, scale=rtab[:, 8, 4 + h:5 + h]),
              reads=[bret, bmisc], writes=[bret])
    P.dve(lambda e: e.tensor_scalar(out=kdc, in0=kdc, scalar1=0.0625, scalar2=None, op0=ALU.mult), reads=[bret], writes=[bret])
    P.dve(lambda e: e.memset(onesf[:], 1.0), writes=[bconst])
    P.dve(lambda e: e.memset(epsb[:], EPS), writes=[bconst])
    P.dma("sp", n1g[:], n1g_d.rearrange("l p c -> p l c"), writes=[bconst])
    P.dma("sp", n2g[:], n2g_d.rearrange("l p c -> p l c"), writes=[bconst])
    P.dma("sp", adab[:], adab_d.rearrange("l p c -> p l c"), writes=[bconst])
    P.dma("sp", cTs[:], cT_d[:], writes=[bconst])
    P.act(lambda e: e.activation(out=cact[:], in_=cTs[:], func=AF.Silu), reads=[bconst], writes=[bconst])

    def adaln():
        cv = Carver()
        wblk = [cv.take([128, 8, 512], BF16) for _ in range(2)]
        bw = [Buf("adaw0"), Buf("adaw1")]
        for l in range(n_layers):
            ps, bps = bank()
            for jb in range(12):
                s = jb % 2
                src = adaw_d[l, jb]
                P.dma("pool", wblk[s][:], src, writes=[bw[s]])
                for jj in range(4):
                    j = jb * 4 + jj
                    for k in range(8):
                        P.pe(lambda e, s=s, k=k, jj=jj, j=j, ps=ps: e.matmul(
                            ps[:, j * R:(j + 1) * R], lhsT=wblk[s][:, k, jj * 128:(jj + 1) * 128],
                            rhs=cact[:, k, :], start=(k == 0), stop=(k == 7)),
                            reads=[bw[s], bconst], writes=[bps])
            P.dve(lambda e, l=l, ps=ps: e.tensor_tensor(
                out=modT[:, l], in0=ps[:, 0:48 * R].rearrange("p (j r) -> p j r", r=R),
                in1=adab[:, l, :].unsqueeze(2).to_broadcast([128, 48, R]), op=ALU.add),
                reads=[bps, bconst], writes=[bmod])
            for which in range(2):
                g = n1g if which == 0 else n2g
                m0 = which * 3
                P.dve(lambda e, l=l, which=which, g=g, m0=m0: e.scalar_tensor_tensor(
                    out=AB[:, l, which, 0], in0=modT[:, l, (m0 + 1) * 8:(m0 + 2) * 8, :], scalar=1.0,
                    in1=g[:, l, :].unsqueeze(2).to_broadcast([128, 8, R]), op0=ALU.add, op1=ALU.mult),
                    reads=[bmod, bconst], writes=[bAB])
                P.dve(lambda e, l=l, which=which, m0=m0: e.tensor_copy(
                    out=AB[:, l, which, 1], in_=modT[:, l, m0 * 8:(m0 + 1) * 8, :]),
                    reads=[bmod], writes=[bAB])

    def gate_bcast(l, which, row, gi):
        m = 2 if which == 0 else 5
        for half in range(2):
            ps, bps = bank()
            for cc in range(4):
                c = half * 4 + cc
                dg = diag[(half * 4 + cc) % 2]
                bd = bdiag[(half * 4 + cc) % 2]
                P.dve(lambda e, dg=dg, c=c: e.tensor_scalar(
                    out=dg[:], in0=identf[:], scalar1=modT[:, l, m * 8 + c, row:row + 1], scalar2=None,
                    op0=ALU.mult), reads=[bconst, bmod], writes=[bd])
                P.pe(lambda e, dg=dg, cc=cc, ps=ps: e.matmul(
                    ps[:, cc * 128:(cc + 1) * 128], lhsT=onesf[:], rhs=dg[:], start=True, stop=True),
                    reads=[bd, bconst], writes=[bps])
            P.act(lambda e, ps=ps, half=half: e.activation(
                out=Gbc[gi][:, half * 512:(half + 1) * 512], in_=ps[:], func=AF.Copy),
                reads=[bps], writes=[bG[gi]])

    diag = [P.sbuf(f"diag{i}", [128, 128], F32) for i in range(2)]
    bdiag = [Buf("diag0"), Buf("diag1")]

    nscr = {}

    def norm_phase(l, which, b, tiles, cv, slots=None, nbuf=2):
        junk = cv.take([128, D], BF16)
        xsb = [cv.take([128, D], BF16) for _ in range(nbuf)]
        xsb = xsb * (2 // nbuf)
        tmpf = [cv.take([128, 8, 128], F32) for _ in range(nbuf)]
        tmpf = tmpf * (2 // nbuf)
        ss = [cv.take([128, 4], F32) for _ in range(2)]
        bj = Buf("junk")
        bxs = [Buf() for _ in range(nbuf)] * (2 // nbuf)
        btf = [Buf() for _ in range(nbuf)] * (2 // nbuf)
        bss = [Buf(), Buf()]
        for i, t in enumerate(tiles):
            s = i % 2
            row = NB if t < 2 else b
            hs = t if slots is None else slots[i]
            P.dve(lambda e, s=s: e.memset(ss[s][:], 0.0), writes=[bss[s]])
            P.act(lambda e, t=t, s=s: e.activation(out=junk[:], in_=X[:, t, :], func=AF.Square,
                                                   accum_out=ss[s][:, 0:1]),
                  reads=[bX[t], bss[s]], writes=[bj, bss[s]])
            P.act(lambda e, s=s: e.activation(out=ss[s][:, 1:2], in_=ss[s][:, 0:1], func=AF.Ln,
                                              scale=1.0 / D, bias=epsb[:, 0:1]),
                  reads=[bss[s], bconst], writes=[bss[s]])
            P.act(lambda e, s=s: e.activation(out=ss[s][:, 2:3], in_=ss[s][:, 1:2], func=AF.Exp, scale=-0.5),
                  reads=[bss[s]], writes=[bss[s]])
            P.dve(lambda e, t=t, s=s: e.tensor_scalar(out=xsb[s][:], in0=X[:, t, :], scalar1=ss[s][:, 2:3],
                                                      scalar2=None, op0=ALU.mult),
                  reads=[bX[t], bss[s]], writes=[bxs[s]])
            ps, bps = bank()
            pT = ps.bitcast(BF16)
            for c in range(8):
                P.pe(lambda e, c=c, s=s, pT=pT: e.transpose(pT[:, c * 128:(c + 1) * 128],
                                                           xsb[s][:, c * 128:(c + 1) * 128], identb[:]),
                     reads=[bxs[s], bconst], writes=[bps])
            P.dve(lambda e, s=s, pT=pT, row=row: e.tensor_tensor(
                out=tmpf[s][:], in0=pT.rearrange("p (c n) -> p c n", c=8),
                in1=AB[:, l, which, 0, :, row:row + 1].to_broadcast([128, 8, 128]), op=ALU.mult),
                reads=[bps, bAB], writes=[btf[s]])
            P.pool(lambda e, s=s, hs=hs, row=row: e.tensor_tensor(
                out=HT[:, :, hs * 128:(hs + 1) * 128], in0=tmpf[s][:],
                in1=AB[:, l, which, 1, :, row:row + 1].to_broadcast([128, 8, 128]), op=ALU.add),
                reads=[btf[s], bAB], writes=[bHT[hs]])

    def x_update(ps, bps, t, n, gi, tmps, btmps, k):
        tm = tmps[k % 2]
        bt = btmps[k % 2]
        P.dve(lambda e: e.tensor_tensor(out=tm[:], in0=ps[:], in1=Gbc[gi][:, n * 512:(n + 1) * 512], op=ALU.mult),
              reads=[bps, bG[gi]], writes=[bt])
        P.pool(lambda e: e.tensor_tensor(out=X[:, t, n * 512:(n + 1) * 512], in0=X[:, t, n * 512:(n + 1) * 512],
                                         in1=tm[:], op=ALU.add),
               reads=[bt, bX[t]], writes=[bX[t]])

    def attention(l, b, tiles):
        cv = Carver()
        Gbc[1] = cv.take([128, D], F32)
        gmark = cv.off
        norm_phase(l, 0, b, tiles, cv)
        gate_bcast(l, 0, b, 0)
        gate_bcast(l, 0, NB, 1)
        P.barrier()
        cv.off = gmark
        acos = cv.take([128, 17, 64], F32)
        asin = cv.take([128, 17, 64], F32)
        G5 = cv.take([128, 320], F32)
        esink = cv.take([128, 16, 128], BF16)
        sinkf = cv.take([128, 16], F32)
        selsink = cv.take([128, 128], BF16)
        maskb = cv.take([128, 2, 512], BF16)
        btab = Buf("atab")
        P.dma("sp", acos[:], dram["acos"][:], writes=[btab])
        P.dma("sp", asin[:], dram["asin"][:], writes=[btab])
        P.dma("sp", G5[:], again_d.partition_broadcast(128), writes=[btab])
        P.dma("sp", sinkf[0:1, :], asink_d[:], writes=[btab])
        P.act(lambda e: e.activation(out=sinkf[0:1, :], in_=sinkf[0:1, :], func=AF.Exp), reads=[btab], writes=[btab])
        P.dve(lambda e: e.tensor_copy(out=esink[0:1], in_=sinkf[0:1, :].unsqueeze(2).to_broadcast([1, 16, 128])),
              reads=[btab], writes=[btab])
        P.dve(lambda e: e.memset(selsink[0:1, 0:64], 0.0), writes=[btab])
        P.dve(lambda e: e.memset(selsink[0:1, 64:128], 1.0), reads=[btab], writes=[btab])
        for mi in range(2):
            P.dve(lambda e, mi=mi: e.tensor_scalar(
                out=maskb[:, mi, :].rearrange("p (h q) -> p h q", h=4),
                in0=masks[:, mi, :].unsqueeze(1).to_broadcast([128, 4, 128]),
                scalar1=30000.0, scalar2=-30000.0, op0=ALU.mult, op1=ALU.add), reads=[bconst, btab], writes=[btab])
        Wg = cv.take([128, 8, 384], BF16)
        WoS = cv.take([64, 4, D], F32)
        WoL = cv.take([64, 4, D], BF16)
        WoC = cv.take([64, 4, D], BF16)
        bW = Buf("Wg")
        bWoS = Buf("WoS")
        bWo = Buf("Wo")
        NS = 8
        KT = cv.take([64, NS, 128], BF16)
        Vx = cv.take([128, NS, 128], BF16)
        bKT = [Buf() for _ in range(NS)]
        bV = [Buf() for _ in range(NS)]
        P.dve(lambda e: e.memset(Vx[:, :, 64:128], 1.0), writes=bV)
        QT = [cv.take([64, 512], BF16) for _ in range(3)]
        bQT = [Buf() for _ in range(3)]
        sq = cv.take([128, 320], F32); bsq = Buf()
        stat = [cv.take([128, 16], F32) for _ in range(2)]; bst = [Buf(), Buf()]
        u = cv.take([128, 320], F32); bu = Buf()
        t1 = cv.take([128, 320], F32); bt1 = Buf()
        t2 = cv.take([128, 320], F32); bt2 = Buf()
        qb = [cv.take([128, 320], BF16) for _ in range(2)]; bqb = [Buf(), Buf()]
        NPT = 5
        PT = [cv.take([128, 512], BF16) for _ in range(NPT)]; bPT = [Buf() for _ in range(NPT)]
        rc = cv.take([128, 512], F32); brc = Buf()
        OT = [cv.take([64, 512], BF16) for _ in range(2)]; bOT = [Buf(), Buf()]
        pt_rr = [0]

        def slot(t):
            return t if t < 2 else 2 + (t - 2) % 6

        for g in range(4):
            P.dma("pool", Wg[:], wqkv_d[g], writes=[bW])
            P.dma("sp", WoS[:], awo_d[g], writes=[bWoS])
            P.dve(lambda e: e.tensor_tensor(out=WoL[:], in0=WoS[:], in1=Gbc[0][0:64, :].unsqueeze(1).to_broadcast([64, 4, D]),
                                            op=ALU.mult), reads=[bWoS, bG[0]], writes=[bWo])
            P.pool(lambda e: e.tensor_tensor(out=WoC[:], in0=WoS[:], in1=Gbc[1][0:64, :].unsqueeze(1).to_broadcast([64, 4, D]),
                                             op=ALU.mult), reads=[bWoS, bG[1]], writes=[bWo])
            last = tiles[-1]
            state = {}
            for k in range(last + 4):
                ts = k if k <= last else None
                ta = k - 2 if 0 <= k - 2 <= last else None
                tw = k - 3 if 0 <= k - 3 <= last else None
                if ts is not None:
                    s = slot(ts)
                    ti = 16 if ts < 2 else ts - 2
                    pj, bpj = bank()
                    for kk in range(8):
                        P.pe(lambda e, kk=kk, ts=ts, pj=pj: e.matmul(pj[:, 0:384], lhsT=HT[:, kk, ts * 128:(ts + 1) * 128],
                                                                     rhs=Wg[:, kk, :], start=(kk == 0), stop=(kk == 7)),
                             reads=[bHT[ts], bW], writes=[bpj])
                    P.act(lambda e, pj=pj: e.activation(out=sq[:], in_=pj[:, 0:320], func=AF.Square), reads=[bpj], writes=[bsq])
                if ta is not None:
                    if ta < 2:
                        chunks = [0, 1]
                    else:
                        chunks = [0, 1] + [tk for tk in (ta - 1, ta, ta + 1) if 2 <= tk <= last]
                    qs = ta % 3
                    pts = []
                    for tk in chunks:
                        sk = slot(tk)
                        pS, bpS = bank()
                        masked = ta >= 2 and tk >= 2 and tk != ta
                        P.pe(lambda e, sk=sk, pS=pS, qs=qs, masked=masked: e.matmul(
                            pS[:], lhsT=KT[:, sk, :], rhs=QT[qs][:], start=True, stop=not masked),
                            reads=[bKT[sk], bQT[qs]], writes=[bpS])
                        if masked:
                            mi = 0 if tk == ta - 1 else 1
                            P.pe(lambda e, pS=pS, mi=mi: e.matmul(pS[:], lhsT=identb[:], rhs=maskb[:, mi, :], start=False, stop=True),
                                 reads=[bconst, btab], writes=[bpS])
                        kq = pt_rr[0] % NPT
                        pt_rr[0] += 1
                        P.act(lambda e, pS=pS, kq=kq: e.activation(out=PT[kq][:], in_=pS[:], func=AF.Exp, scale=0.125),
                              reads=[bpS], writes=[bPT[kq]])
                        pts.append((sk, kq))
                if ts is not None:
                    st = stat[k % 2]
                    bs = bst[k % 2]
                    q_ = qb[k % 2]
                    bq_ = bqb[k % 2]
                    P.dve(lambda e, st=st: e.tensor_reduce(out=st[:, 0:5], in_=sq[:].rearrange("p (h d) -> p h d", d=64),
                                                           axis=AX.X, op=ALU.add), reads=[bsq], writes=[bs])
                    P.act(lambda e, st=st: e.activation(out=st[:, 5:10], in_=st[:, 0:5], func=AF.Ln, scale=1.0 / 64,
                                                        bias=epsb[:, 0:1]), reads=[bs, bconst], writes=[bs])
                    P.act(lambda e, st=st: e.activation(out=st[:, 10:15], in_=st[:, 5:10], func=AF.Exp, scale=-0.5),
                          reads=[bs], writes=[bs])
                    P.dve(lambda e, pj=pj: e.tensor_tensor(out=u[:], in0=pj[:, 0:320], in1=G5[:], op=ALU.mult),
                          reads=[bpj, btab], writes=[bu])
                    P.act(lambda e, pj=pj, s=s: e.activation(out=Vx[:, s, 0:64], in_=pj[:, 320:384], func=AF.Copy),
                          reads=[bpj], writes=[bV[s]])
                    u3 = u[:].rearrange("p (h d) -> p h d", d=64)
                    P.pool(lambda e, ti=ti, u3=u3: e.tensor_tensor(out=t1[:].rearrange("p (h d) -> p h d", d=64), in0=u3,
                                                                   in1=acos[:, ti, :].unsqueeze(1).to_broadcast([128, 5, 64]), op=ALU.mult),
                           reads=[bu, btab], writes=[bt1])
                    u5 = u[:].rearrange("p (h a f i) -> p (h a) f i", a=2, f=2, i=16)
                    t25 = t2[:].rearrange("p (h a f i) -> p (h a) f i", a=2, f=2, i=16)
                    sn4 = asin[:, ti, :].rearrange("p (a f i) -> p a f i", a=2, f=2)
                    for f in range(2):
                        for a in range(2):
                            P.dve(lambda e, f=f, a=a, u5=u5, t25=t25, sn4=sn4: e.tensor_tensor(
                                out=t25[:, a::2, f, :], in0=u5[:, a::2, 1 - f, :],
                                in1=sn4[:, a, f, :].unsqueeze(1).to_broadcast([128, 5, 16]), op=ALU.mult),
                                reads=[bu, btab], writes=[bt2])
                    P.pool(lambda e: e.tensor_tensor(out=t1[:], in0=t1[:], in1=t2[:], op=ALU.add),
                           reads=[bt1, bt2], writes=[bt1])
                if tw is not None:
                    o = OT[tw % 2]
                    Wsel = WoC if tw < 2 else WoL
                    for n in range(2):
                        pw, bpw = bank()
                        for hh in range(4):
                            P.pe(lambda e, hh=hh, n=n, pw=pw, o=o, Wsel=Wsel: e.matmul(
                                pw[:], lhsT=o[:, hh * 128:(hh + 1) * 128], rhs=Wsel[:, hh, n * 512:(n + 1) * 512],
                                start=(hh == 0), stop=(hh == 3)), reads=[bOT[tw % 2], bWo], writes=[bpw])
                        P.dve(lambda e, tw=tw, n=n, pw=pw: e.tensor_tensor(
                            out=X[:, tw, n * 512:(n + 1) * 512], in0=X[:, tw, n * 512:(n + 1) * 512], in1=pw[:], op=ALU.add),
                            reads=[bpw, bX[tw]], writes=[bX[tw]])
                if ta is not None:
                    po, bpo = bank()
                    for ci, (sk, kq) in enumerate(pts):
                        P.pe(lambda e, sk=sk, kq=kq, ci=ci, po=po: e.matmul(po[:], lhsT=Vx[:, sk, :], rhs=PT[kq][:],
                                                                            start=(ci == 0), stop=False),
                             reads=[bV[sk], bPT[kq]], writes=[bpo])
                    P.pe(lambda e, po=po, g=g: e.matmul(po[:], lhsT=selsink[0:1, :], rhs=esink[0:1, 4 * g:4 * g + 4, :],
                                                        start=False, stop=True), reads=[btab], writes=[bpo])
                    P.act(lambda e, po=po: e.activation(out=rc[64:128, :], in_=po[64:128, :], func=AF.Ln), reads=[bpo], writes=[brc])
                    P.act(lambda e: e.activation(out=rc[64:128, :], in_=rc[64:128, :], func=AF.Exp, scale=-1.0), reads=[brc], writes=[brc])
                if ts is not None:
                    P.dve(lambda e, st=st, q_=q_: e.tensor_tensor(out=q_[:].rearrange("p (h d) -> p h d", d=64),
                                                                  in0=t1[:].rearrange("p (h d) -> p h d", d=64),
                                                                  in1=st[:, 10:15].unsqueeze(2).to_broadcast([128, 5, 64]), op=ALU.mult),
                          reads=[bt1, bs], writes=[bq_])
                    pq, bpq = bank()
                    pqb = pq.bitcast(BF16)
                    for h in range(5):
                        P.pe(lambda e, h=h, pqb=pqb, q_=q_: e.transpose(pqb[0:64, h * 128:(h + 1) * 128], q_[:, h * 64:(h + 1) * 64], identb[:]),
                             reads=[bq_, bconst], writes=[bpq])
                    P.act(lambda e, pqb=pqb, ts=ts: e.activation(out=QT[ts % 3][:], in_=pqb[0:64, 0:512], func=AF.Copy),
                          reads=[bpq], writes=[bQT[ts % 3]])
                    P.dve(lambda e, pqb=pqb, s=s: e.tensor_copy(out=KT[:, s, :], in_=pqb[0:64, 512:640]), reads=[bpq], writes=[bKT[s]])
                if ta is not None:
                    o2 = OT[ta % 2]
                    P.dve(lambda e, po=po, o2=o2: e.tensor_tensor(out=o2[:], in0=po[0:64, :], in1=rc[64:128, :], op=ALU.mult),
                          reads=[bpo, brc], writes=[bOT[ta % 2]])

    bsbd = [Buf(f"sbd{i}") for i in range(16)]

    def retention(l, b):
        cv = Carver()
        norm_phase(l, 0, b, list(range(NT)), cv)
        gate_bcast(l, 0, b, 0)
        P.barrier()
        cv = Carver()
        Wkv = cv.take([128, 8, 768], BF16); bWkv = Buf("Wkv")
        Wqg = cv.take([128, 8, 768], BF16); bWqg = Buf("Wqg")
        Wo = cv.take([128, 4, D], BF16); bWo = Buf("Wo_r")
        gnb = cv.take([128, 512], F32); bgnb = Buf("gnb")
        RC = cv.take([128, 17, 64], F32)
        RS = cv.take([128, 17, 64], F32)
        brt = Buf("rtabs")
        P.dma("sp", RC[:], dram["rcos"][:], writes=[brt])
        P.dma("sp", RS[:], dram["rsin"][:], writes=[brt])
        Sm = cv.take([128, 2, 512], F32); bSm = [Buf(), Buf()]
        Sfb = cv.take([128, 2, 512], BF16); bSfb = [Buf(), Buf()]
        Sld = [cv.take([128, 2, 512], BF16) for _ in range(2)]; bSld = [Buf(), Buf()]
        fA = cv.take([128, 512], F32); bfA = Buf()
        fB = cv.take([128, 512], F32); bfB = Buf()
        fC = cv.take([128, 512], F32); bfC = Buf()
        fD = [cv.take([128, 512], F32) for _ in range(2)]; bfD = [Buf(), Buf()]
        yn = cv.take([128, 512], F32); byn = Buf()
        tmps = [cv.take([128, 512], F32)] * 2; btmps = [Buf()] * 2
        qkb = cv.take([128, 1024], BF16); bqkb = Buf()
        kdec = [cv.take([128, 256], BF16) for _ in range(2)]; bkdec = [Buf(), Buf()]
        vbf = [cv.take([128, 512], BF16) for _ in range(2)]; bvbf = [Buf(), Buf()]
        QKT = [cv.take([128, 8, 128], BF16) for _ in range(2)]; bQKT = [Buf(), Buf()]
        sD = cv.take([128, 128], BF16); bsD = Buf()
        yy = cv.take([128, 2, 512], BF16); byy = Buf()
        ybf = yy[:, 0, :]
        yT = yy[:, 1, :].rearrange("p (a b) -> p a b", a=4)
        Sst = yy
        st = cv.take([128, 16], F32); bst = Buf()
        upd = [0]

        def rope(src, bsrc, nq, tl):
            W = nq * 256
            sv = src.rearrange("p (q a f i) -> p q a f i", a=2, f=2, i=64)
            t1v = fA[:, 0:W].rearrange("p (q a f i) -> p q a f i", a=2, f=2, i=64)
            uv = fB[:, 0:W].rearrange("p (q a f i) -> p q a f i", a=2, f=2, i=64)
            rv = fC[:, 0:W].rearrange("p (q a f i) -> p q a f i", a=2, f=2, i=64)
            for a in range(2):
                sa = tl if a == 0 else 16
                P.dve(lambda e, a=a, sa=sa: e.tensor_tensor(
                    out=t1v[:, :, a], in0=sv[:, :, a],
                    in1=RC[:, sa, :].unsqueeze(1).unsqueeze(1).to_broadcast([128, nq, 2, 64]), op=ALU.mult),
                    reads=[bsrc, brt], writes=[bfA])
                for f in range(2):
                    P.dve(lambda e, a=a, f=f, sa=sa: e.tensor_tensor(
                        out=uv[:, :, a, f], in0=sv[:, :, a, 1 - f],
                        in1=RS[:, sa, :].unsqueeze(1).to_broadcast([128, nq, 64]), op=ALU.mult),
                        reads=[bsrc, brt], writes=[bfB])
            P.pool(lambda e: e.tensor_tensor(out=rv[:, :, :, 0], in0=t1v[:, :, :, 0], in1=uv[:, :, :, 0], op=ALU.subtract),
                   reads=[bfA, bfB], writes=[bfC])
            P.pool(lambda e: e.tensor_tensor(out=rv[:, :, :, 1], in0=t1v[:, :, :, 1], in1=uv[:, :, :, 1], op=ALU.add),
                   reads=[bfA, bfB], writes=[bfC])

        def state_update(h, d, sl):
            for dc in range(2):
                ps, bps = bank()
                P.pe(lambda e, dc=dc, ps=ps: e.matmul(ps[:], lhsT=kdec[sl][:, dc * 128:(dc + 1) * 128], rhs=vbf[sl][:],
                                                      start=True, stop=True), reads=[bkdec[sl], bvbf[sl]], writes=[bps])
                P.dve(lambda e, dc=dc, ps=ps: e.scalar_tensor_tensor(
                    out=Sm[:, dc, :], in0=Sm[:, dc, :], scalar=rtab[:, 8, 16 + d * 4 + h:17 + d * 4 + h], in1=ps[:],
                    op0=ALU.mult, op1=ALU.add), reads=[bps, bSm[dc], bret], writes=[bSm[dc]])

        def head_pass(h):
            P.dma("pool", Wkv[:], rw_d[h][:, :, 0:768], writes=[bWkv])
            P.dma("pool", Wqg[:], rw_d[h][:, :, 768:1536], writes=[bWqg])
            P.dma("pool", Wo[:], rwo_d[h], writes=[bWo])
            P.dma("sp", gnb[:], rgn_d[0:1, h * 512:(h + 1) * 512].partition_broadcast(128), writes=[bgnb])
            P.phase(f'b{b}_ret_h{h}_B')
            P.dve(lambda e: e.memset(Sm[:], 0.0), writes=bSm)
            orderB = [1, 0] + list(range(17, 1, -1))

            def stageA_B(i, t):
                sl = i % 2
                pk, bpk = bank()
                pv, bpv = bank()
                for k in range(8):
                    P.pe(lambda e, k=k, pk=pk: e.matmul(pk[:, 0:256], lhsT=HT[:, k, t * 128:(t + 1) * 128], rhs=Wkv[:, k, 0:256],
                                                        start=(k == 0), stop=(k == 7)), reads=[bHT[t], bWkv], writes=[bpk])
                for k in range(8):
                    P.pe(lambda e, k=k, pv=pv: e.matmul(pv[:], lhsT=HT[:, k, t * 128:(t + 1) * 128], rhs=Wkv[:, k, 256:768],
                                                        start=(k == 0), stop=(k == 7)), reads=[bHT[t], bWkv], writes=[bpv])
                P.act(lambda e: e.activation(out=vbf[sl][:], in_=pv[:], func=AF.Copy), reads=[bpv], writes=[bvbf[sl]])
                if t >= 2:
                    rope(pk[:, 0:256], bpk, 1, t - 2)
                    P.dve(lambda e: e.tensor_scalar(out=kdec[sl][:], in0=fC[:, 0:256], scalar1=rtab[:, 8, 12 + h:13 + h],
                                                    scalar2=None, op0=ALU.mult), reads=[bfC, bret], writes=[bkdec[sl]])
                else:
                    P.dve(lambda e: e.tensor_scalar(out=kdec[sl][:], in0=pk[:, 0:256], scalar1=rtab[:, 8, 12 + h:13 + h],
                                                    scalar2=None, op0=ALU.mult), reads=[bpk, bret], writes=[bkdec[sl]])

            def stageB_B(i, t):
                if t >= 2:
                    P.act(lambda e: e.activation(out=Sst[:], in_=Sm[:], func=AF.Copy), reads=bSm, writes=[byy])
                    P.dma("sp", sb_d[t - 2], Sst[:].rearrange("p a b -> p (a b)"), reads=[byy], writes=[bsbd[t - 2]])
                state_update(h, 1, i % 2)

            for i in range(len(orderB) + 1):
                if i < len(orderB):
                    stageA_B(i, orderB[i])
                if i >= 1:
                    stageB_B(i - 1, orderB[i - 1])
            P.phase(f'b{b}_ret_h{h}_F')
            P.dve(lambda e: e.memset(Sm[:], 0.0), writes=bSm)
            P.pool(lambda e: e.memset(Sfb[:], 0.0), writes=bSfb)
            P.dma("sp", Sld[0][:].rearrange("p a b -> p (a b)"), sb_d[0], reads=[bsbd[0]], writes=[bSld[0]])
            ctxF = {}
            ctxA = {}

            def stageA1_F(t):
                sl = t % 2
                lat = t >= 2
                pqk, bpqk = bank()
                pv, bpv = bank()
                if lat:
                    for k in range(8):
                        P.pe(lambda e, k=k: e.matmul(pqk[:, 0:256], lhsT=HT[:, k, t * 128:(t + 1) * 128], rhs=Wqg[:, k, 0:256],
                                                     start=(k == 0), stop=(k == 7)), reads=[bHT[t], bWqg], writes=[bpqk])
                for k in range(8):
                    P.pe(lambda e, k=k: e.matmul(pqk[:, 256:512], lhsT=HT[:, k, t * 128:(t + 1) * 128], rhs=Wkv[:, k, 0:256],
                                                 start=(k == 0), stop=(k == 7)), reads=[bHT[t], bWkv], writes=[bpqk])
                for k in range(8):
                    P.pe(lambda e, k=k: e.matmul(pv[:], lhsT=HT[:, k, t * 128:(t + 1) * 128], rhs=Wkv[:, k, 256:768],
                                                 start=(k == 0), stop=(k == 7)), reads=[bHT[t], bWkv], writes=[bpv])
                P.act(lambda e: e.activation(out=vbf[sl][:], in_=pv[:], func=AF.Copy), reads=[bpv], writes=[bvbf[sl]])
                if lat:
                    pg, bpg = bank()
                    for k in range(8):
                        P.pe(lambda e, k=k: e.matmul(pg[:], lhsT=HT[:, k, t * 128:(t + 1) * 128], rhs=Wqg[:, k, 256:768],
                                                     start=(k == 0), stop=(k == 7)), reads=[bHT[t], bWqg], writes=[bpg])
                    P.act(lambda e: e.activation(out=fD[sl][:], in_=pg[:], func=AF.Silu), reads=[bpg], writes=[bfD[sl]])
                ctxA[t] = (pqk, bpqk)

            def stageA1b_F(t):
                sl = t % 2
                lat = t >= 2
                pqk, bpqk = ctxA.pop(t)
                if lat:
                    rope(pqk[:], bpqk, 2, t - 2)
                    P.act(lambda e: e.activation(out=qkb[:, 0:512], in_=fC[:], func=AF.Copy), reads=[bfC], writes=[bqkb])
                    P.dve(lambda e: e.tensor_scalar(out=kdec[sl][:], in0=fC[:, 256:512], scalar1=rtab[:, 8, 8 + h:9 + h],
                                                    scalar2=None, op0=ALU.mult), reads=[bfC, bret], writes=[bkdec[sl]])
                    for d in range(2):
                        P.dve(lambda e, d=d: e.tensor_scalar(
                            out=qkb[:, 512 + 256 * d:768 + 256 * d], in0=fC[:, 0:256],
                            scalar1=rtab[:, 8, 24 + 4 * d + h:25 + 4 * d + h], scalar2=None, op0=ALU.mult),
                            reads=[bfC, bret, bqkb], writes=[bqkb])
                else:
                    P.dve(lambda e: e.tensor_scalar(out=kdec[sl][:], in0=pqk[:, 256:512], scalar1=rtab[:, 8, 8 + h:9 + h],
                                                    scalar2=None, op0=ALU.mult), reads=[bpqk, bret], writes=[bkdec[sl]])

            def stageA2_F(t):
                if t < 2:
                    return
                sl = t % 2
                if t + 1 < NT:
                    s2 = (t + 1) % 2
                    P.dma("sp", Sld[s2][:].rearrange("p a b -> p (a b)"), sb_d[t - 1], reads=[bsbd[t - 1]], writes=[bSld[s2]])
                pT, bpT = bank()
                pTb = pT.bitcast(BF16)
                for c in range(8):
                    P.pe(lambda e, c=c: e.transpose(pTb[:, c * 128:(c + 1) * 128], qkb[:, c * 128:(c + 1) * 128], identb[:]),
                         reads=[bqkb, bconst], writes=[bpT])
                P.act(lambda e: e.activation(out=QKT[sl][:, 0:4, :].rearrange("p a b -> p (a b)"), in_=pTb[:, 0:512], func=AF.Copy),
                      reads=[bpT], writes=[bQKT[sl]])
                P.dve(lambda e: e.tensor_copy(out=QKT[sl][:, 4:8, :].rearrange("p a b -> p (a b)"), in_=pTb[:, 512:1024]),
                      reads=[bpT, bQKT[sl]], writes=[bQKT[sl]])

            def stageB1_F(t):
                sl = t % 2
                if t >= 2:
                    pss, bpss = bank()
                    for dc in range(2):
                        P.pe(lambda e, dc=dc: e.matmul(pss[:, 0:128], lhsT=QKT[sl][:, 2 + dc, :], rhs=QKT[sl][:, dc, :],
                                                       start=(dc == 0), stop=(dc == 1)), reads=[bQKT[sl]], writes=[bpss])
                    P.dve(lambda e: e.tensor_tensor(out=sD[:], in0=pss[:, 0:128], in1=DT[:, h, :], op=ALU.mult),
                          reads=[bpss, bret], writes=[bsD])
                    po, bpo = bank()
                    ctxF[t] = (po, bpo)
                    P.pe(lambda e: e.matmul(po[:], lhsT=sD[:], rhs=vbf[sl][:], start=True, stop=False),
                         reads=[bsD, bvbf[sl]], writes=[bpo])
                    for dc in range(2):
                        P.pe(lambda e, dc=dc: e.matmul(po[:], lhsT=QKT[sl][:, 4 + dc, :], rhs=Sfb[:, dc, :], start=False, stop=False),
                             reads=[bQKT[sl], bSfb[dc]], writes=[bpo])
                    for dc in range(2):
                        P.pe(lambda e, dc=dc: e.matmul(po[:], lhsT=QKT[sl][:, 6 + dc, :], rhs=Sld[sl][:, dc, :],
                                                       start=False, stop=(dc == 1)),
                             reads=[bQKT[sl], bSld[sl]], writes=[bpo])
                state_update(h, 0, sl)
                if t + 1 < NT:
                    for dc in range(2):
                        P.act(lambda e, dc=dc: e.activation(out=Sfb[:, dc, :], in_=Sm[:, dc, :], func=AF.Copy),
                              reads=[bSm[dc]], writes=[bSfb[dc]])

            def stageB2_F(t):
                if t < 2:
                    return
                sl = t % 2
                po, bpo = ctxF.pop(t)
                P.dve(lambda e: e.bn_stats(out=st[:, 0:6], in_=po[:]), reads=[bpo], writes=[bst])
                P.dve(lambda e: e.bn_aggr(out=st[:, 6:8], in_=st[:, 0:6]), reads=[bst], writes=[bst])
                P.act(lambda e: e.activation(out=st[:, 8:9], in_=st[:, 7:8], func=AF.Ln, scale=1.0, bias=epsb[:, 0:1]),
                      reads=[bst, bconst], writes=[bst])
                P.act(lambda e: e.activation(out=st[:, 9:10], in_=st[:, 8:9], func=AF.Exp, scale=-0.5), reads=[bst], writes=[bst])
                P.dve(lambda e: e.scalar_tensor_tensor(out=st[:, 10:11], in0=st[:, 6:7], scalar=-1.0, in1=st[:, 9:10],
                                                       op0=ALU.mult, op1=ALU.mult), reads=[bst], writes=[bst])
                P.act(lambda e: e.activation(out=yn[:], in_=po[:], func=AF.Identity, scale=st[:, 9:10], bias=st[:, 10:11]),
                      reads=[bpo, bst], writes=[byn])
                P.pool(lambda e: e.tensor_tensor(out=yn[:], in0=yn[:], in1=gnb[:], op=ALU.mult), reads=[byn, bgnb], writes=[byn])
                P.dve(lambda e: e.tensor_tensor(out=ybf, in0=yn[:], in1=fD[sl][:], op=ALU.mult), reads=[byn, bfD[sl]], writes=[byy])

            def stageB2b_F(t):
                if t < 2:
                    return
                sl = t % 2
                pY, bpY = bank()
                pYb = pY.bitcast(BF16)
                for c in range(4):
                    P.pe(lambda e, c=c: e.transpose(pYb[:, c * 128:(c + 1) * 128], ybf[:, c * 128:(c + 1) * 128], identb[:]),
                         reads=[byy, bconst], writes=[bpY])
                P.act(lambda e: e.activation(out=yy[:, 1, :], in_=pYb[:, 0:512], func=AF.Copy), reads=[bpY], writes=[byy])
                for n in range(2):
                    pw, bpw = bank()
                    for c in range(4):
                        P.pe(lambda e, c=c, n=n, pw=pw: e.matmul(pw[:], lhsT=yT[:, c, :], rhs=Wo[:, c, n * 512:(n + 1) * 512],
                                                                 start=(c == 0), stop=(c == 3)), reads=[byy, bWo], writes=[bpw])
                    x_update(pw, bpw, t, n, 0, tmps, btmps, upd[0])
                    upd[0] += 1

            for i in range(NT + 1):
                if i >= 1:
                    stageB1_F(i - 1)
                    stageB2_F(i - 1)
                if i < NT:
                    stageA1_F(i)
                if i >= 1:
                    stageB2b_F(i - 1)
                if i < NT:
                    stageA1b_F(i)
                    stageA2_F(i)


        for h_ in range(4):
            head_pass(h_)

    def ffn(l, b, tiles):
        cv = Carver()
        has_ctx = tiles[0] < 2
        if has_ctx:
            Gbc[1] = cv.take([128, D], F32)
        gate_bcast(l, 1, b, 0)
        if has_ctx:
            gate_bcast(l, 1, NB, 1)
        Wout = cv.take([128, NJ, D], BF16)
        bWout = Buf("Wout")
        Win = [cv.take([128, 2, 8, 128], BF16) for _ in range(3)]
        bWin = [Buf(), Buf(), Buf()]
        GT = 6
        hidA = cv.take([128, 6, GT * 128], BF16)
        bhid = [Buf() for _ in range(NJ)]
        sg = [cv.take([128, 512], F32) for _ in range(2)]
        bsg = [Buf(), Buf()]
        tmps = sg; btmps = bsg
        ncv = Carver()
        ncv.off = cv.off

        def hid_ap(j, c0, c1):
            if j < 6:
                return hidA[:, j, c0:c1]
            jj = j - 6
            base = GT * 128 + (jj % 2) * GT * 128
            return HT[:, jj // 2, base + c0:base + c1]

        for q in range(11):
            P.dma("pool", Wout[:, q * 2:(q + 1) * 2, :], wout_d[l][:, q * 2:(q + 1) * 2, :], writes=[bWout])
        ng = (len(tiles) + GT - 1) // GT
        base_, extra = divmod(len(tiles), ng)
        groups = []
        i0 = 0
        for gi_ in range(ng):
            n_ = base_ + (1 if gi_ < extra else 0)
            groups.append(tiles[i0:i0 + n_])
            i0 += n_
        wi = 0
        k2 = 0
        upd = 0
        for ts in groups:
            nt_ = len(ts)
            ncv.off = cv.off
            norm_phase(l, 1, b, ts, ncv, slots=list(range(nt_)), nbuf=1)
            P.barrier()
            nb_ = (nt_ + 2) // 3
            bb, be = divmod(nt_, nb_)
            blocks = []
            c = 0
            for bi in range(nb_):
                n_ = bb + (1 if bi < be else 0)
                blocks.append((c, n_))
                c += n_
            for j in range(NJ):
                s = wi % 3
                wi += 1
                P.dma("pool", Win[s][:], win_d[l, j], writes=[bWin[s]])
                for (c0t, nbt) in blocks:
                    tok0 = c0t * 128
                    ntok = nbt * 128
                    pg, bpg = bank()
                    pu, bpu = bank()
                    for gu, (pp, bpp) in enumerate(((pg, bpg), (pu, bpu))):
                        for k in range(8):
                            P.pe(lambda e, k=k, gu=gu, pp=pp, s=s, ntok=ntok, tok0=tok0: e.matmul(
                                pp[:, 0:ntok], lhsT=Win[s][:, gu, k, :],
                                rhs=HT[:, k, tok0:tok0 + ntok], start=(k == 0), stop=(k == 7)),
                                reads=[bWin[s]] + [bHT[c0t + i] for i in range(nbt)], writes=[bpp])
                    q = k2 % 2
                    k2 += 1
                    P.act(lambda e, q=q, pg=pg, ntok=ntok: e.activation(out=sg[q][:, 0:ntok], in_=pg[:, 0:ntok], func=AF.Silu),
                          reads=[bpg], writes=[bsg[q]])
                    P.dve(lambda e, q=q, pu=pu, j=j, ntok=ntok, tok0=tok0: e.tensor_tensor(
                        out=hid_ap(j, tok0, tok0 + ntok), in0=sg[q][:, 0:ntok], in1=pu[:, 0:ntok], op=ALU.mult),
                        reads=[bsg[q], bpu], writes=[bhid[j]])
            for ti, t in enumerate(ts):
                gi = 1 if t < 2 else 0
                for n in range(2):
                    pw, bpw = bank()
                    for j in range(NJ):
                        P.pe(lambda e, j=j, n=n, pw=pw, ti=ti: e.matmul(
                            pw[:], lhsT=hid_ap(j, ti * 128, (ti + 1) * 128), rhs=Wout[:, j, n * 512:(n + 1) * 512],
                            start=(j == 0), stop=(j == NJ - 1)),
                            reads=[bhid[j], bWout], writes=[bpw])
                    x_update(pw, bpw, t, n, gi, tmps, btmps, upd)
                    upd += 1

    def finish():
        P.barrier()
        P.emit()
        return nc, P

    P.barrier()
    adaln()
    P.barrier()
    if stage == 1:
        P.dma("sp", dbg_d[:, 0:48 * R], modT[:, 0].rearrange("p j r -> p (j r)"), reads=[bmod])
        return finish()
    for b in range(NB):
        for t in range(2):
            P.dma("sp", X[:, t, :], ctx_d[b, t * 128:(t + 1) * 128, :], writes=[bX[t]])
        for t in range(2, NT):
            P.dma("sp", X[:, t, :], x_d[b, (t - 2) * 128:(t - 1) * 128, :], writes=[bX[t]])
        tiles = list(range(NT))
        P.phase(f'b{b}_attn')
        if stage == 2:
            cv = Carver()
            norm_phase(0, 0, b, [0, 1, 2, 3], cv)
            gate_bcast(0, 0, b, 0)
            P.barrier()
            tmpd = cv.take([128, 1024], F32)
            P.dve(lambda e: e.tensor_copy(out=tmpd[:, 0:512], in_=HT[:, 0:4, 0:128]))
            P.dve(lambda e: e.tensor_copy(out=tmpd[:, 512:1024], in_=HT[:, 0:4, 384:512]))
            P.barrier()
            P.dma("sp", dbg_d[:, 0:1024], tmpd[:])
            P.dma("sp", dbg_d[:, 1024:2048], Gbc[0][:])
            return finish()
        if stage == 5:
            retention(1, b)
            P.barrier()
            for t in range(2, NT):
                P.dma("sp", y_d[b, (t - 2) * 128:(t - 1) * 128, :], X[:, t, :], reads=[bX[t]])
            return finish()
        if stage != 4:
            attention(0, b, tiles)
        P.barrier()
        if stage == 3:
            for t in range(2, NT):
                P.dma("sp", y_d[b, (t - 2) * 128:(t - 1) * 128, :], X[:, t, :], reads=[bX[t]])
            return finish()
        P.phase(f'b{b}_ffn0')
        ffn(0, b, tiles)
        P.barrier()
        if n_layers > 1:
            P.phase(f'b{b}_ret')
            retention(1, b)
            P.barrier()
            P.phase(f'b{b}_ffn1')
            ffn(1, b, list(range(2, NT)))
            P.barrier()
        P.phase(f'b{b}_out')
        for t in range(2, NT):
            P.dma("sp", y_d[b, (t - 2) * 128:(t - 1) * 128, :], X[:, t, :], reads=[bX[t]])
    P.barrier()
    P.emit()
    return nc, P


def _c(a):
    return np.ascontiguousarray(a, dtype=np.float32)


def prep_shared(inp):
    sh = {}
    ada_w = np.asarray(inp["ada_w"], np.float32)
    sh["ada_w_r"] = _c(ada_w.reshape(2, 8, 128, 12, 512).transpose(0, 3, 2, 1, 4))
    sh["ada_bT"] = _c(np.asarray(inp["ada_b"]).reshape(2, 48, 128).transpose(0, 2, 1))
    sh["norm1_gT"] = _c(np.asarray(inp["norm1_g"]).reshape(2, 8, 128).transpose(0, 2, 1))
    sh["norm2_gT"] = _c(np.asarray(inp["norm2_g"]).reshape(2, 8, 128).transpose(0, 2, 1))
    w_in = np.asarray(inp["ffn_w_in"], np.float32).reshape(2, 8, 128, 2, NJ, 128)
    sh["ffn_w_in_r"] = _c(w_in.transpose(0, 4, 2, 3, 1, 5))
    w_out = np.asarray(inp["ffn_w_out"], np.float32).reshape(2, NJ, 128, D)
    sh["ffn_w_out_r"] = _c(w_out.transpose(0, 2, 1, 3))
    wqkv = np.asarray(inp["attn_w_qkv"], np.float32)[0]
    parts = []
    for g in range(4):
        blk = np.concatenate([wqkv[:, g * 256:(g + 1) * 256], wqkv[:, 1024 + g * 64:1024 + (g + 1) * 64],
                              wqkv[:, 1280 + g * 64:1280 + (g + 1) * 64]], axis=1)
        parts.append(blk.reshape(8, 128, 384).transpose(1, 0, 2))
    sh["attn_w_qkv_r"] = _c(np.stack(parts))
    awo = np.asarray(inp["attn_w_o"], np.float32)[0]
    sh["attn_w_o_r"] = _c(awo.reshape(4, 4, 64, D).transpose(0, 2, 1, 3))
    gq = np.asarray(inp["attn_q_norm"], np.float32)[0]
    gk = np.asarray(inp["attn_k_norm"], np.float32)[0]
    sh["attn_gains"] = _c(np.concatenate([gq, gq, gq, gq, gk])[None, :])
    sh["attn_sink"] = _c(np.asarray(inp["attn_sink"]).reshape(1, 16))
    rw = np.asarray(inp["ret_w_qkvg"], np.float32)[0]
    parts = []
    for h in range(4):
        blk = np.concatenate([rw[:, 1024 + h * 256:1024 + (h + 1) * 256], rw[:, 2048 + h * 512:2048 + (h + 1) * 512],
                              rw[:, h * 256:(h + 1) * 256], rw[:, 4096 + h * 512:4096 + (h + 1) * 512]], axis=1)
        parts.append(blk.reshape(8, 128, 1536).transpose(1, 0, 2))
    sh["ret_w_r"] = _c(np.stack(parts))
    sh["ret_decay_logit"] = _c(np.asarray(inp["ret_decay_logit"]).reshape(1, 8))
    sh["ret_gn_g"] = _c(np.asarray(inp["ret_gn_g"]).reshape(1, 2048))
    sh["ret_w_o_r"] = _c(np.asarray(inp["ret_w_o"], np.float32)[0].reshape(4, 4, 128, D).transpose(0, 2, 1, 3))
    sh.update(make_consts())
    return sh


def core_inputs(inp, sh, b0, NB):
    m = dict(sh)
    m["x"] = _c(np.asarray(inp["x"])[b0:b0 + NB])
    m["ctx"] = _c(np.asarray(inp["ctx"])[b0:b0 + NB])
    cc = np.concatenate([np.asarray(inp["c"], np.float32)[b0:b0 + NB], np.asarray(inp["c_ctx"], np.float32)[None, :]], axis=0)
    m["cT"] = _c(cc.reshape(NB + 1, 8, 128).transpose(2, 1, 0))
    return m


def kernel(**inp):
    NB = 4
    nc, _ = build_program(NB)
    sh = prep_shared(inp)
    in_maps = [core_inputs(inp, sh, c * NB, NB) for c in range(8)]
    res = run_bass_kernel_spmd(nc, in_maps, core_ids=list(range(8)))
    return np.concatenate([r["y"] for r in res.results], axis=0).astype(np.float32)
```

```python
import contextlib
import numpy as np
import concourse.bass as bass
import concourse.mybir as mybir
from concourse.bass_utils import run_bass_kernel_spmd

F32 = mybir.dt.float32
BF16 = mybir.dt.bfloat16
ALU = mybir.AluOpType
AF = mybir.ActivationFunctionType
AX = mybir.AxisListType

SEM_LIMIT = 30000
N_DMA_SEMS = 6
D = 1024
S = 2048
LCTX = 256
NT = 18
DFF = 2816
NJ = 22
EPS = 1e-6


class Buf:
    __slots__ = ("name", "lw", "rd")

    def __init__(self, name=""):
        self.name = name
        self.lw = None
        self.rd = {}


class Prog:
    ENGS = ("pe", "act", "dve", "pool", "sp")

    def __init__(self, nc):
        self.nc = nc
        self.stack = contextlib.ExitStack()
        self.ops = {e: [] for e in self.ENGS}
        self.cnt = {e: 0 for e in self.ENGS}
        self.sem = {}
        self.waited = {e: {} for e in self.ENGS}
        self.pending = {e: {} for e in self.ENGS}
        self.last_ev = {e: None for e in self.ENGS}
        self.dma_out = []
        self.nsem = 0
        for e in self.ENGS:
            self.sem[e] = self._new_sem(e)
        self.dma_sems = {}
        self.dma_cnt = {}
        self.dma_rr = {}
        for q in ("sp", "pool", "act"):
            self.dma_sems[q] = [self._new_sem("d" + q) for _ in range(N_DMA_SEMS)]
            self.dma_cnt[q] = [0] * N_DMA_SEMS
            self.dma_rr[q] = 0
        self.n_ops = 0
        self.limit = None
        self.marks = {}
        self.phases = []

    def mark(self, name):
        self.marks.setdefault(name, self.n_ops)

    def phase(self, name):
        self.phases.append((name, len(self.ops['pe']), len(self.ops['act']), len(self.ops['dve']), len(self.ops['pool'])))

    def _new_sem(self, tag):
        self.nsem += 1
        return self.stack.enter_context(self.nc.semaphore(f"s_{tag}_{self.nsem}"))

    def sbuf(self, name, shape, dt):
        return self.stack.enter_context(self.nc.sbuf_tensor("sb_" + name, list(shape), dt))

    def psum(self, name, shape, dt):
        return self.stack.enter_context(self.nc.psum_tensor(name, list(shape), dt))

    def op(self, eng, fn, reads=(), writes=(), dma=False):
        if self.limit is not None and self.n_ops >= self.limit:
            return None
        waits = self.pending[eng]
        self.pending[eng] = {}
        wd = self.waited[eng]
        for s in list(waits.keys()):
            if wd.get(s, 0) >= waits[s]:
                del waits[s]

        def need(ev):
            if ev is None:
                return
            sem, val = ev
            if eng == "pe" and not dma and sem is self.sem["pe"]:
                return
            if wd.get(sem, 0) >= val:
                return
            if waits.get(sem, 0) < val:
                waits[sem] = val

        for b in reads:
            need(b.lw)
        for b in writes:
            need(b.lw)
            for s, v in b.rd.items():
                need((s, v))
        if dma:
            i = self.dma_rr[eng]
            self.dma_rr[eng] = (i + 1) % N_DMA_SEMS
            if self.dma_cnt[eng][i] * 16 + 16 > SEM_LIMIT:
                self.dma_sems[eng][i] = self._new_sem("d" + eng)
                self.dma_cnt[eng][i] = 0
            sem = self.dma_sems[eng][i]
            need((sem, self.dma_cnt[eng][i] * 16))
            self.dma_cnt[eng][i] += 1
            ev = (sem, self.dma_cnt[eng][i] * 16)
            inc = 16
            self.dma_out.append(ev)
        else:
            if self.cnt[eng] + 1 > SEM_LIMIT:
                self.sem[eng] = self._new_sem(eng)
                self.cnt[eng] = 0
            self.cnt[eng] += 1
            ev = (self.sem[eng], self.cnt[eng])
            inc = 1
            self.last_ev[eng] = ev
        for s, v in waits.items():
            wd[s] = v
        self.ops[eng].append((list(waits.items()), fn, ev[0], inc))
        for b in reads:
            if b.rd.get(ev[0], 0) < ev[1]:
                b.rd[ev[0]] = ev[1]
        for b in writes:
            b.lw = ev
            b.rd = {}
        self.n_ops += 1
        return ev

    def pe(self, fn, reads=(), writes=()):
        return self.op("pe", fn, reads, writes)

    def act(self, fn, reads=(), writes=()):
        return self.op("act", fn, reads, writes)

    def dve(self, fn, reads=(), writes=()):
        return self.op("dve", fn, reads, writes)

    def pool(self, fn, reads=(), writes=()):
        return self.op("pool", fn, reads, writes)

    def dma(self, q, out, in_, reads=(), writes=(), **kw):
        return self.op(q, lambda e: e.dma_start(out=out, in_=in_, **kw), reads, writes, dma=True)

    def barrier(self):
        evs = [ev for ev in self.last_ev.values() if ev is not None] + self.dma_out
        self.dma_out = []
        for e in self.ENGS:
            p = self.pending[e]
            for s, v in evs:
                if p.get(s, 0) < v:
                    p[s] = v

    def emit(self):
        nc = self.nc
        with nc.Block() as block:
            def replay(name):
                def run(e):
                    for waits, fn, sem, inc in self.ops[name]:
                        for s, v in waits:
                            e.wait_ge(s, v)
                        ins = fn(e)
                        ins.then_inc(sem, inc)
                    for s, v in self.pending[name].items():
                        e.wait_ge(s, v)
                return run

            block.tensor(replay("pe"))
            block.scalar(replay("act"))
            block.vector(replay("dve"))
            block.gpsimd(replay("pool"))
            block.sync(replay("sp"))
        self.stack.close()


def make_consts():
    c = {}
    p = np.arange(128)
    inv = 10000.0 ** (-np.arange(0, 32, 2, dtype=np.float32) / 32.0)
    acos = np.ones((128, 17, 64), np.float32)
    asin = np.zeros((128, 17, 64), np.float32)
    for tl in range(16):
        pos = tl * 128 + p
        row = (pos // 64).astype(np.float32)
        col = (pos % 64).astype(np.float32)
        ar = row[:, None] * inv[None, :]
        ac = col[:, None] * inv[None, :]
        acos[:, tl, :] = np.concatenate([np.cos(ar), np.cos(ar), np.cos(ac), np.cos(ac)], axis=1)
        asin[:, tl, :] = np.concatenate([-np.sin(ar), np.sin(ar), -np.sin(ac), np.sin(ac)], axis=1)
    c["acos"] = acos
    c["asin"] = asin
    inv2 = 10000.0 ** (-np.arange(0, 128, 2, dtype=np.float32) / 128.0)
    rcos = np.ones((128, 17, 64), np.float32)
    rsin = np.zeros((128, 17, 64), np.float32)
    for tl in range(16):
        pos = tl * 128 + p
        row = (pos // 64).astype(np.float32)
        ar = row[:, None] * inv2[None, :]
        rcos[:, tl, :] = np.cos(ar)
        rsin[:, tl, :] = np.sin(ar)
    colp = (p % 64).astype(np.float32)
    ac = colp[:, None] * inv2[None, :]
    rcos[:, 16, :] = np.cos(ac)
    rsin[:, 16, :] = np.sin(ac)
    c["rcos"] = rcos
    c["rsin"] = rsin
    m = p[:, None]
    n = p[None, :]
    misc = np.zeros((128, 10, 128), np.float32)
    misc[:, 0, :] = (m == n)
    misc[:, 1, :] = (m >= n)
    misc[:, 2, :] = (m <= n)
    misc[:, 3, :] = np.maximum(n - m, 0)
    misc[:, 4, :] = np.maximum(m - n, 0)
    misc[:, 5, :] = (n >= m)
    misc[:, 6, :] = (m >= n)
    misc[:, 7, :] = n + 1.0
    misc[:, 8, :] = 128.0 - n
    misc[:, 9, 0] = 127.0 - p
    misc[:, 9, 1] = p
    misc[:, 9, 2] = 1.0
    misc[:, 9, 3] = p + 1.0
    misc[:, 9, 4] = 128.0 - p
    c["misc"] = misc
    return c


CONST_SHAPES = {"acos": [128, 17, 64], "asin": [128, 17, 64], "rcos": [128, 17, 64],
                "rsin": [128, 17, 64], "misc": [128, 10, 128]}


def build_program(NB, n_layers=2, stage=None, limit=None):
    nc = bass.Bass("TRN2", target_bir_lowering=False)
    R = NB + 1
    dram = {}

    def din(name, shape):
        dram[name] = nc.dram_tensor(name, list(shape), F32, kind="ExternalInput").ap()
        return dram[name]

    x_d = din("x", [NB, S, D])
    ctx_d = din("ctx", [NB, LCTX, D])
    cT_d = din("cT", [128, 8, R])
    adaw_d = din("ada_w_r", [2, 12, 128, 8, 512])
    adab_d = din("ada_bT", [2, 128, 48])
    n1g_d = din("norm1_gT", [2, 128, 8])
    n2g_d = din("norm2_gT", [2, 128, 8])
    win_d = din("ffn_w_in_r", [2, NJ, 128, 2, 8, 128])
    wout_d = din("ffn_w_out_r", [2, 128, NJ, D])
    wqkv_d = din("attn_w_qkv_r", [4, 128, 8, 384])
    awo_d = din("attn_w_o_r", [4, 64, 4, D])
    again_d = din("attn_gains", [1, 320])
    asink_d = din("attn_sink", [1, 16])
    rw_d = din("ret_w_r", [4, 128, 8, 1536])
    rdl_d = din("ret_decay_logit", [1, 8])
    rgn_d = din("ret_gn_g", [1, 2048])
    rwo_d = din("ret_w_o_r", [4, 128, 4, D])
    for k, shp in CONST_SHAPES.items():
        din(k, shp)
    y_d = nc.dram_tensor("y", [NB, S, D], F32, kind="ExternalOutput").ap()
    dbg_d = nc.dram_tensor("dbg", [128, 2048], F32, kind="ExternalOutput").ap() if stage else None
    sb_d = nc.dram_tensor("sb_scratch", [16, 128, 1024], BF16, kind="ExternalOutput").ap()

    P = Prog(nc)
    P.limit = limit

    X = P.sbuf("X", [128, NT, D], F32)
    bX = [Buf(f"X{t}") for t in range(NT)]
    HT = P.sbuf("HT", [128, 8, NT * 128], BF16)
    bHT = [Buf(f"HT{t}") for t in range(NT)]
    identf = P.sbuf("identf", [128, 128], F32)
    rtab = P.sbuf("rtab", [128, 9, 128], F32)
    DT = P.sbuf("DT", [128, 4, 128], F32)
    bret = Buf("ret_tabs")
    identb = P.sbuf("identb", [128, 128], BF16)
    masks = P.sbuf("masks", [128, 2, 128], BF16)
    onesf = P.sbuf("onesf", [128, 128], F32)
    epsb = P.sbuf("epsb", [128, 1], F32)
    bconst = Buf("const")
    modT = P.sbuf("modT", [128, 2, 48, R], F32)
    bmod = Buf("modT")
    AB = P.sbuf("AB", [128, 2, 2, 2, 8, R], F32)
    bAB = Buf("AB")
    n1g = P.sbuf("n1g", [128, 2, 8], F32)
    n2g = P.sbuf("n2g", [128, 2, 8], F32)
    adab = P.sbuf("adab", [128, 2, 48], F32)
    cact = P.sbuf("cact", [128, 8, R], BF16)
    cTs = P.sbuf("cTs", [128, 8, R], F32)
    Gbc0 = P.sbuf("Gbc0", [128, D], F32)
    Gbc = [Gbc0, None]
    bG = [Buf("G0"), Buf("G1")]
    ARENA_B = 81 * 1024 + 512
    arena = P.sbuf("arena", [128, ARENA_B // 2], BF16)

    class Carver:
        def __init__(self):
            self.off = 0

        def take(self, shape, dt):
            n = int(np.prod(shape[1:]))
            nb = n * (4 if dt == F32 else 2)
            nb = (nb + 63) // 64 * 64
            assert self.off + nb <= ARENA_B, ("arena overflow", self.off + nb)
            a = arena[0:shape[0], self.off // 2:(self.off + nb) // 2]
            self.off += nb
            if dt == F32:
                a = a.bitcast(F32)
            a = a[:, 0:n]
            if len(shape) == 3:
                a = a.rearrange("p (a b) -> p a b", a=shape[1])
            elif len(shape) == 4:
                a = a.rearrange("p (a b c) -> p a b c", a=shape[1], b=shape[2])
            return a

    banks = [P.psum(f"bank{i}", [128, 512], F32) for i in range(8)]
    bbank = [Buf(f"bank{i}") for i in range(8)]
    bank_rr = [0]

    def bank():
        i = bank_rr[0]
        bank_rr[0] = (i + 1) % 8
        return banks[i], bbank[i]

    cv0 = Carver()
    misc = cv0.take([128, 10, 128], F32)
    bmisc = Buf("misc")
    P.dma("sp", misc[:], dram["misc"][:], writes=[bmisc])
    P.dve(lambda e: e.tensor_copy(out=identb[:], in_=misc[:, 0, :]), reads=[bmisc], writes=[bconst])
    P.dve(lambda e: e.tensor_copy(out=identf[:], in_=misc[:, 0, :]), reads=[bmisc], writes=[bconst])
    P.dve(lambda e: e.tensor_copy(out=masks[:], in_=misc[:, 1:3, :]), reads=[bmisc], writes=[bconst])
    dl = cv0.take([128, 8], F32)
    sc = cv0.take([128, 2, 128], F32)
    bsc = Buf("sc")
    P.dma("sp", dl[:], rdl_d.partition_broadcast(128), writes=[bret])
    lgc = rtab[:, 8, 0:8]
    kdc = rtab[:, 8, 8:16]
    cdc = rtab[:, 8, 16:24]
    P.act(lambda e: e.activation(out=dl[:], in_=dl[:], func=AF.Exp, scale=-1.0), reads=[bret], writes=[bret])
    P.act(lambda e: e.activation(out=dl[:], in_=dl[:], func=AF.Ln, scale=1.0, bias=misc[:, 9, 2:3]), reads=[bret, bmisc], writes=[bret])
    P.dve(lambda e: e.tensor_scalar(out=lgc, in0=dl[:], scalar1=-1.0, scalar2=None, op0=ALU.mult), reads=[bret], writes=[bret])
    P.act(lambda e: e.activation(out=cdc, in_=lgc, func=AF.Exp, scale=128.0), reads=[bret], writes=[bret])
    for h in range(4):
        P.act(lambda e, h=h: e.activation(out=sc[:, 0, :], in_=misc[:, 3, :], func=AF.Exp, scale=rtab[:, 8, h:h + 1]),
              reads=[bret, bmisc, bsc], writes=[bsc])
        P.act(lambda e, h=h: e.activation(out=sc[:, 1, :], in_=misc[:, 4, :], func=AF.Exp, scale=rtab[:, 8, 4 + h:5 + h]),
              reads=[bret, bmisc, bsc], writes=[bsc])
        P.dve(lambda e: e.scalar_tensor_tensor(out=sc[:, 0, :], in0=sc[:, 0, :], scalar=0.0625, in1=misc[:, 5, :],
                                               op0=ALU.mult, op1=ALU.mult), reads=[bsc, bmisc], writes=[bsc])
        P.dve(lambda e: e.scalar_tensor_tensor(out=sc[:, 1, :], in0=sc[:, 1, :], scalar=0.0625, in1=misc[:, 6, :],
                                               op0=ALU.mult, op1=ALU.mult), reads=[bsc, bmisc], writes=[bsc])
        P.dve(lambda e, h=h: e.tensor_tensor(out=DT[:, h, :], in0=sc[:, 0, :], in1=sc[:, 1, :], op=ALU.add),
              reads=[bsc], writes=[bret])
        P.act(lambda e, h=h: e.activation(out=rtab[:, 8, 24 + h:25 + h], in_=misc[:, 9, 3:4], func=AF.Exp, scale=rtab[:, 8, h:h + 1]),
              reads=[bret, bmisc], writes=[bret])
        P.act(lambda e, h=h: e.activation(out=rtab[:, 8, 28 + h:29 + h], in_=misc[:, 9, 4:5], func=AF.Exp, scale=rtab[:, 8, 4 + h:5 + h]),
              reads=[bret, bmisc], writes=[bret])
        P.act(lambda e, h=h: e.activation(out=rtab[:, 8, 8 + h:9 + h], in_=misc[:, 9, 0:1], func=AF.Exp, scale=rtab[:, 8, h:h + 1]),
              reads=[bret, bmisc], writes=[bret])
        P.act(lambda e, h=h: e.activation(out=rtab[:, 8, 12 + h:13 + h], in_=misc[:, 9, 1:2], func=AF.Exp, scale=rtab[:, 8, 4 + h:5 + h]),
              reads=[bret, bmisc], writes=[bret])
    P.dve(lambda e: e.tensor_scalar(out=kdc, in0=kdc, scalar1=0.0625, scalar2=None, op0=ALU.mult), reads=[bret], writes=[bret])
    P.dve(lambda e: e.memset(onesf[:], 1.0), writes=[bconst])
    P.dve(lambda e: e.memset(epsb[:], EPS), writes=[bconst])
    P.dma("sp", n1g[:], n1g_d.rearrange("l p c -> p l c"), writes=[bconst])
    P.dma("sp", n2g[:], n2g_d.rearrange("l p c -> p l c"), writes=[bconst])
    P.dma("sp", adab[:], adab_d.rearrange("l p c -> p l c"), writes=[bconst])
    P.dma("sp", cTs[:], cT_d[:], writes=[bconst])
    P.act(lambda e: e.activation(out=cact[:], in_=cTs[:], func=AF.Silu), reads=[bconst], writes=[bconst])

    def adaln():
        cv = Carver()
        wblk = [cv.take([128, 8, 512], BF16) for _ in range(2)]
        bw = [Buf("adaw0"), Buf("adaw1")]
        for l in range(n_layers):
            ps, bps = bank()
            for jb in range(12):
                s = jb % 2
                src = adaw_d[l, jb]
                P.dma("pool", wblk[s][:], src, writes=[bw[s]])
                for jj in range(4):
                    j = jb * 4 + jj
                    for k in range(8):
                        P.pe(lambda e, s=s, k=k, jj=jj, j=j, ps=ps: e.matmul(
                            ps[:, j * R:(j + 1) * R], lhsT=wblk[s][:, k, jj * 128:(jj + 1) * 128],
                            rhs=cact[:, k, :], start=(k == 0), stop=(k == 7)),
                            reads=[bw[s], bconst], writes=[bps])
            P.dve(lambda e, l=l, ps=ps: e.tensor_tensor(
                out=modT[:, l], in0=ps[:, 0:48 * R].rearrange("p (j r) -> p j r", r=R),
                in1=adab[:, l, :].unsqueeze(2).to_broadcast([128, 48, R]), op=ALU.add),
                reads=[bps, bconst], writes=[bmod])
            for which in range(2):
                g = n1g if which == 0 else n2g
                m0 = which * 3
                P.dve(lambda e, l=l, which=which, g=g, m0=m0: e.scalar_tensor_tensor(
                    out=AB[:, l, which, 0], in0=modT[:, l, (m0 + 1) * 8:(m0 + 2) * 8, :], scalar=1.0,
                    in1=g[:, l, :].unsqueeze(2).to_broadcast([128, 8, R]), op0=ALU.add, op1=ALU.mult),
                    reads=[bmod, bconst], writes=[bAB])
                P.dve(lambda e, l=l, which=which, m0=m0: e.tensor_copy(
                    out=AB[:, l, which, 1], in_=modT[:, l, m0 * 8:(m0 + 1) * 8, :]),
                    reads=[bmod], writes=[bAB])

    def gate_bcast(l, which, row, gi):
        m = 2 if which == 0 else 5
        for half in range(2):
            ps, bps = bank()
            for cc in range(4):
                c = half * 4 + cc
                dg = diag[(half * 4 + cc) % 2]
                bd = bdiag[(half * 4 + cc) % 2]
                P.dve(lambda e, dg=dg, c=c: e.tensor_scalar(
                    out=dg[:], in0=identf[:], scalar1=modT[:, l, m * 8 + c, row:row + 1], scalar2=None,
                    op0=ALU.mult), reads=[bconst, bmod], writes=[bd])
                P.pe(lambda e, dg=dg, cc=cc, ps=ps: e.matmul(
                    ps[:, cc * 128:(cc + 1) * 128], lhsT=onesf[:], rhs=dg[:], start=True, stop=True),
                    reads=[bd, bconst], writes=[bps])
            P.act(lambda e, ps=ps, half=half: e.activation(
                out=Gbc[gi][:, half * 512:(half + 1) * 512], in_=ps[:], func=AF.Copy),
                reads=[bps], writes=[bG[gi]])

    diag = [P.sbuf(f"diag{i}", [128, 128], F32) for i in range(2)]
    bdiag = [Buf("diag0"), Buf("diag1")]

    nscr = {}

    def norm_phase(l, which, b, tiles, cv, slots=None, nbuf=2):
        junk = cv.take([128, D], BF16)
        xsb = [cv.take([128, D], BF16) for _ in range(nbuf)]
        xsb = xsb * (2 // nbuf)
        tmpf = [cv.take([128, 8, 128], F32) for _ in range(nbuf)]
        tmpf = tmpf * (2 // nbuf)
        ss = [cv.take([128, 4], F32) for _ in range(2)]
        bj = Buf("junk")
        bxs = [Buf() for _ in range(nbuf)] * (2 // nbuf)
        btf = [Buf() for _ in range(nbuf)] * (2 // nbuf)
        bss = [Buf(), Buf()]
        for i, t in enumerate(tiles):
            s = i % 2
            row = NB if t < 2 else b
            hs = t if slots is None else slots[i]
            P.dve(lambda e, s=s: e.memset(ss[s][:], 0.0), writes=[bss[s]])
            P.act(lambda e, t=t, s=s: e.activation(out=junk[:], in_=X[:, t, :], func=AF.Square,
                                                   accum_out=ss[s][:, 0:1]),
                  reads=[bX[t], bss[s]], writes=[bj, bss[s]])
            P.act(lambda e, s=s: e.activation(out=ss[s][:, 1:2], in_=ss[s][:, 0:1], func=AF.Ln,
                                              scale=1.0 / D, bias=epsb[:, 0:1]),
                  reads=[bss[s], bconst], writes=[bss[s]])
            P.act(lambda e, s=s: e.activation(out=ss[s][:, 2:3], in_=ss[s][:, 1:2], func=AF.Exp, scale=-0.5),
                  reads=[bss[s]], writes=[bss[s]])
            P.dve(lambda e, t=t, s=s: e.tensor_scalar(out=xsb[s][:], in0=X[:, t, :], scalar1=ss[s][:, 2:3],
                                                      scalar2=None, op0=ALU.mult),
                  reads=[bX[t], bss[s]], writes=[bxs[s]])
            ps, bps = bank()
            pT = ps.bitcast(BF16)
            for c in range(8):
                P.pe(lambda e, c=c, s=s, pT=pT: e.transpose(pT[:, c * 128:(c + 1) * 128],
                                                           xsb[s][:, c * 128:(c + 1) * 128], identb[:]),
                     reads=[bxs[s], bconst], writes=[bps])
            P.dve(lambda e, s=s, pT=pT, row=row: e.tensor_tensor(
                out=tmpf[s][:], in0=pT.rearrange("p (c n) -> p c n", c=8),
                in1=AB[:, l, which, 0, :, row:row + 1].to_broadcast([128, 8, 128]), op=ALU.mult),
                reads=[bps, bAB], writes=[btf[s]])
            P.pool(lambda e, s=s, hs=hs, row=row: e.tensor_tensor(
                out=HT[:, :, hs * 128:(hs + 1) * 128], in0=tmpf[s][:],
                in1=AB[:, l, which, 1, :, row:row + 1].to_broadcast([128, 8, 128]), op=ALU.add),
                reads=[btf[s], bAB], writes=[bHT[hs]])

    def x_update(ps, bps, t, n, gi, tmps, btmps, k):
        tm = tmps[k % 2]
        bt = btmps[k % 2]
        P.dve(lambda e: e.tensor_tensor(out=tm[:], in0=ps[:], in1=Gbc[gi][:, n * 512:(n + 1) * 512], op=ALU.mult),
              reads=[bps, bG[gi]], writes=[bt])
        P.pool(lambda e: e.tensor_tensor(out=X[:, t, n * 512:(n + 1) * 512], in0=X[:, t, n * 512:(n + 1) * 512],
                                         in1=tm[:], op=ALU.add),
               reads=[bt, bX[t]], writes=[bX[t]])

    def attention(l, b, tiles):
        cv = Carver()
        Gbc[1] = cv.take([128, D], F32)
        gmark = cv.off
        norm_phase(l, 0, b, tiles, cv)
        gate_bcast(l, 0, b, 0)
        gate_bcast(l, 0, NB, 1)
        P.barrier()
        cv.off = gmark
        acos = cv.take([128, 17, 64], F32)
        asin = cv.take([128, 17, 64], F32)
        G5 = cv.take([128, 320], F32)
        esink = cv.take([128, 16, 128], BF16)
        sinkf = cv.take([128, 16], F32)
        selsink = cv.take([128, 128], BF16)
        maskb = cv.take([128, 2, 512], BF16)
        btab = Buf("atab")
        P.dma("sp", acos[:], dram["acos"][:], writes=[btab])
        P.dma("sp", asin[:], dram["asin"][:], writes=[btab])
        P.dma("sp", G5[:], again_d.partition_broadcast(128), writes=[btab])
        P.dma("sp", sinkf[0:1, :], asink_d[:], writes=[btab])
        P.act(lambda e: e.activation(out=sinkf[0:1, :], in_=sinkf[0:1, :], func=AF.Exp), reads=[btab], writes=[btab])
        P.dve(lambda e: e.tensor_copy(out=esink[0:1], in_=sinkf[0:1, :].unsqueeze(2).to_broadcast([1, 16, 128])),
              reads=[btab], writes=[btab])
        P.dve(lambda e: e.memset(selsink[0:1, 0:64], 0.0), writes=[btab])
        P.dve(lambda e: e.memset(selsink[0:1, 64:128], 1.0), reads=[btab], writes=[btab])
        for mi in range(2):
            P.dve(lambda e, mi=mi: e.tensor_scalar(
                out=maskb[:, mi, :].rearrange("p (h q) -> p h q", h=4),
                in0=masks[:, mi, :].unsqueeze(1).to_broadcast([128, 4, 128]),
                scalar1=30000.0, scalar2=-30000.0, op0=ALU.mult, op1=ALU.add), reads=[bconst, btab], writes=[btab])
        Wg = cv.take([128, 8, 384], BF16)
        WoS = cv.take([64, 4, D], F32)
        WoL = cv.take([64, 4, D], BF16)
        WoC = cv.take([64, 4, D], BF16)
        bW = Buf("Wg")
        bWoS = Buf("WoS")
        bWo = Buf("Wo")
        NS = 8
        KT = cv.take([64, NS, 128], BF16)
        Vx = cv.take([128, NS, 128], BF16)
        bKT = [Buf() for _ in range(NS)]
        bV = [Buf() for _ in range(NS)]
        P.dve(lambda e: e.memset(Vx[:, :, 64:128], 1.0), writes=bV)
        QT = [cv.take([64, 512], BF16) for _ in range(3)]
        bQT = [Buf() for _ in range(3)]
        sq = cv.take([128, 320], F32); bsq = Buf()
        stat = [cv.take([128, 16], F32) for _ in range(2)]; bst = [Buf(), Buf()]
        u = cv.take([128, 320], F32); bu = Buf()
        t1 = cv.take([128, 320], F32); bt1 = Buf()
        t2 = cv.take([128, 320], F32); bt2 = Buf()
        qb = [cv.take([128, 320], BF16) for _ in range(2)]; bqb = [Buf(), Buf()]
        NPT = 5
        PT = [cv.take([128, 512], BF16) for _ in range(NPT)]; bPT = [Buf() for _ in range(NPT)]
        rc = cv.take([128, 512], F32); brc = Buf()
        OT = [cv.take([64, 512], BF16) for _ in range(2)]; bOT = [Buf(), Buf()]
        pt_rr = [0]

        def slot(t):
            return t if t < 2 else 2 + (t - 2) % 6

        for g in range(4):
            P.dma("pool", Wg[:], wqkv_d[g], writes=[bW])
            P.dma("sp", WoS[:], awo_d[g], writes=[bWoS])
            P.dve(lambda e: e.tensor_tensor(out=WoL[:], in0=WoS[:], in1=Gbc[0][0:64, :].unsqueeze(1).to_broadcast([64, 4, D]),
                                            op=ALU.mult), reads=[bWoS, bG[0]], writes=[bWo])
            P.pool(lambda e: e.tensor_tensor(out=WoC[:], in0=WoS[:], in1=Gbc[1][0:64, :].unsqueeze(1).to_broadcast([64, 4, D]),
                                             op=ALU.mult), reads=[bWoS, bG[1]], writes=[bWo])
            last = tiles[-1]
            state = {}
            for k in range(last + 4):
                ts = k if k <= last else None
                ta = k - 2 if 0 <= k - 2 <= last else None
                tw = k - 3 if 0 <= k - 3 <= last else None
                if ts is not None:
                    s = slot(ts)
                    ti = 16 if ts < 2 else ts - 2
                    pj, bpj = bank()
                    for kk in range(8):
                        P.pe(lambda e, kk=kk, ts=ts, pj=pj: e.matmul(pj[:, 0:384], lhsT=HT[:, kk, ts * 128:(ts + 1) * 128],
                                                                     rhs=Wg[:, kk, :], start=(kk == 0), stop=(kk == 7)),
                             reads=[bHT[ts], bW], writes=[bpj])
                    P.act(lambda e, pj=pj: e.activation(out=sq[:], in_=pj[:, 0:320], func=AF.Square), reads=[bpj], writes=[bsq])
                if ta is not None:
                    if ta < 2:
                        chunks = [0, 1]
                    else:
                        chunks = [0, 1] + [tk for tk in (ta - 1, ta, ta + 1) if 2 <= tk <= last]
                    qs = ta % 3
                    pts = []
                    for tk in chunks:
                        sk = slot(tk)
                        pS, bpS = bank()
                        masked = ta >= 2 and tk >= 2 and tk != ta
                        P.pe(lambda e, sk=sk, pS=pS, qs=qs, masked=masked: e.matmul(
                            pS[:], lhsT=KT[:, sk, :], rhs=QT[qs][:], start=True, stop=not masked),
                            reads=[bKT[sk], bQT[qs]], writes=[bpS])
                        if masked:
                            mi = 0 if tk == ta - 1 else 1
                            P.pe(lambda e, pS=pS, mi=mi: e.matmul(pS[:], lhsT=identb[:], rhs=maskb[:, mi, :], start=False, stop=True),
                                 reads=[bconst, btab], writes=[bpS])
                        kq = pt_rr[0] % NPT
                        pt_rr[0] += 1
                        P.act(lambda e, pS=pS, kq=kq: e.activation(out=PT[kq][:], in_=pS[:], func=AF.Exp, scale=0.125),
                              reads=[bpS], writes=[bPT[kq]])
                        pts.append((sk, kq))
                if ts is not None:
                    st = stat[k % 2]
                    bs = bst[k % 2]
                    q_ = qb[k % 2]
                    bq_ = bqb[k % 2]
                    P.dve(lambda e, st=st: e.tensor_reduce(out=st[:, 0:5], in_=sq[:].rearrange("p (h d) -> p h d", d=64),
                                                           axis=AX.X, op=ALU.add), reads=[bsq], writes=[bs])
                    P.act(lambda e, st=st: e.activation(out=st[:, 5:10], in_=st[:, 0:5], func=AF.Ln, scale=1.0 / 64,
                                                        bias=epsb[:, 0:1]), reads=[bs, bconst], writes=[bs])
                    P.act(lambda e, st=st: e.activation(out=st[:, 10:15], in_=st[:, 5:10], func=AF.Exp, scale=-0.5),
                          reads=[bs], writes=[bs])
                    P.dve(lambda e, pj=pj: e.tensor_tensor(out=u[:], in0=pj[:, 0:320], in1=G5[:], op=ALU.mult),
                          reads=[bpj, btab], writes=[bu])
                    P.act(lambda e, pj=pj, s=s: e.activation(out=Vx[:, s, 0:64], in_=pj[:, 320:384], func=AF.Copy),
                          reads=[bpj], writes=[bV[s]])
                    u3 = u[:].rearrange("p (h d) -> p h d", d=64)
                    P.pool(lambda e, ti=ti, u3=u3: e.tensor_tensor(out=t1[:].rearrange("p (h d) -> p h d", d=64), in0=u3,
                                                                   in1=acos[:, ti, :].unsqueeze(1).to_broadcast([128, 5, 64]), op=ALU.mult),
                           reads=[bu, btab], writes=[bt1])
                    u5 = u[:].rearrange("p (h a f i) -> p (h a) f i", a=2, f=2, i=16)
                    t25 = t2[:].rearrange("p (h a f i) -> p (h a) f i", a=2, f=2, i=16)
                    sn4 = asin[:, ti, :].rearrange("p (a f i) -> p a f i", a=2, f=2)
                    for f in range(2):
                        for a in range(2):
                            P.dve(lambda e, f=f, a=a, u5=u5, t25=t25, sn4=sn4: e.tensor_tensor(
                                out=t25[:, a::2, f, :], in0=u5[:, a::2, 1 - f, :],
                                in1=sn4[:, a, f, :].unsqueeze(1).to_broadcast([128, 5, 16]), op=ALU.mult),
                                reads=[bu, btab], writes=[bt2])
                    P.pool(lambda e: e.tensor_tensor(out=t1[:], in0=t1[:], in1=t2[:], op=ALU.add),
                           reads=[bt1, bt2], writes=[bt1])
                if tw is not None:
                    o = OT[tw % 2]
                    Wsel = WoC if tw < 2 else WoL
                    for n in range(2):
                        pw, bpw = bank()
                        for hh in range(4):
                            P.pe(lambda e, hh=hh, n=n, pw=pw, o=o, Wsel=Wsel: e.matmul(
                                pw[:], lhsT=o[:, hh * 128:(hh + 1) * 128], rhs=Wsel[:, hh, n * 512:(n + 1) * 512],
                                start=(hh == 0), stop=(hh == 3)), reads=[bOT[tw % 2], bWo], writes=[bpw])
                        P.dve(lambda e, tw=tw, n=n, pw=pw: e.tensor_tensor(
                            out=X[:, tw, n * 512:(n + 1) * 512], in0=X[:, tw, n * 512:(n + 1) * 512], in1=pw[:], op=ALU.add),
                            reads=[bpw, bX[tw]], writes=[bX[tw]])
                if ta is not None:
                    po, bpo = bank()
                    for ci, (sk, kq) in enumerate(pts):
                        P.pe(lambda e, sk=sk, kq=kq, ci=ci, po=po: e.matmul(po[:], lhsT=Vx[:, sk, :], rhs=PT[kq][:],
                                                                            start=(ci == 0), stop=False),
                             reads=[bV[sk], bPT[kq]], writes=[bpo])
                    P.pe(lambda e, po=po, g=g: e.matmul(po[:], lhsT=selsink[0:1, :], rhs=esink[0:1, 4 * g:4 * g + 4, :],
                                                        start=False, stop=True), reads=[btab], writes=[bpo])
                    P.act(lambda e, po=po: e.activation(out=rc[64:128, :], in_=po[64:128, :], func=AF.Ln), reads=[bpo], writes=[brc])
                    P.act(lambda e: e.activation(out=rc[64:128, :], in_=rc[64:128, :], func=AF.Exp, scale=-1.0), reads=[brc], writes=[brc])
                if ts is not None:
                    P.dve(lambda e, st=st, q_=q_: e.tensor_tensor(out=q_[:].rearrange("p (h d) -> p h d", d=64),
                                                                  in0=t1[:].rearrange("p (h d) -> p h d", d=64),
                                                                  in1=st[:, 10:15].unsqueeze(2).to_broadcast([128, 5, 64]), op=ALU.mult),
                          reads=[bt1, bs], writes=[bq_])
                    pq, bpq = bank()
                    pqb = pq.bitcast(BF16)
                    for h in range(5):
                        P.pe(lambda e, h=h, pqb=pqb, q_=q_: e.transpose(pqb[0:64, h * 128:(h + 1) * 128], q_[:, h * 64:(h + 1) * 64], identb[:]),
                             reads=[bq_, bconst], writes=[bpq])
                    P.act(lambda e, pqb=pqb, ts=ts: e.activation(out=QT[ts % 3][:], in_=pqb[0:64, 0:512], func=AF.Copy),
                          reads=[bpq], writes=[bQT[ts % 3]])
                    P.dve(lambda e, pqb=pqb, s=s: e.tensor_copy(out=KT[:, s, :], in_=pqb[0:64, 512:640]), reads=[bpq], writes=[bKT[s]])
                if ta is not None:
                    o2 = OT[ta % 2]
                    P.dve(lambda e, po=po, o2=o2: e.tensor_tensor(out=o2[:], in0=po[0:64, :], in1=rc[64:128, :], op=ALU.mult),
                          reads=[bpo, brc], writes=[bOT[ta % 2]])

    bsbd = [Buf(f"sbd{i}") for i in range(16)]

    def retention(l, b):
        cv = Carver()
        norm_phase(l, 0, b, list(range(NT)), cv)
        gate_bcast(l, 0, b, 0)
        P.barrier()
        cv = Carver()
        Wkv = cv.take([128, 8, 768], BF16); bWkv = Buf("Wkv")
        Wqg = cv.take([128, 8, 768], BF16); bWqg = Buf("Wqg")
        Wo = cv.take([128, 4, D], BF16); bWo = Buf("Wo_r")
        gnb = cv.take([128, 512], F32); bgnb = Buf("gnb")
        RC = cv.take([128, 17, 64], F32)
        RS = cv.take([128, 17, 64], F32)
        brt = Buf("rtabs")
        P.dma("sp", RC[:], dram["rcos"][:], writes=[brt])
        P.dma("sp", RS[:], dram["rsin"][:], writes=[brt])
        Sm = cv.take([128, 2, 512], F32); bSm = [Buf(), Buf()]
        Sfb = cv.take([128, 2, 512], BF16); bSfb = [Buf(), Buf()]
        Sld = [cv.take([128, 2, 512], BF16) for _ in range(2)]; bSld = [Buf(), Buf()]
        fA = cv.take([128, 512], F32); bfA = Buf()
        fB = cv.take([128, 512], F32); bfB = Buf()
        fC = cv.take([128, 512], F32); bfC = Buf()
        fD = [cv.take([128, 512], F32) for _ in range(2)]; bfD = [Buf(), Buf()]
        yn = cv.take([128, 512], F32); byn = Buf()
        tmps = [cv.take([128, 512], F32)] * 2; btmps = [Buf()] * 2
        qkb = cv.take([128, 1024], BF16); bqkb = Buf()
        kdec = [cv.take([128, 256], BF16) for _ in range(2)]; bkdec = [Buf(), Buf()]
        vbf = [cv.take([128, 512], BF16) for _ in range(2)]; bvbf = [Buf(), Buf()]
        QKT = [cv.take([128, 8, 128], BF16) for _ in range(2)]; bQKT = [Buf(), Buf()]
        sD = cv.take([128, 128], BF16); bsD = Buf()
        yy = cv.take([128, 2, 512], BF16); byy = Buf()
        ybf = yy[:, 0, :]
        yT = yy[:, 1, :].rearrange("p (a b) -> p a b", a=4)
        Sst = yy
        st = cv.take([128, 16], F32); bst = Buf()
        upd = [0]

        def rope(src, bsrc, nq, tl):
            W = nq * 256
            sv = src.rearrange("p (q a f i) -> p q a f i", a=2, f=2, i=64)
            t1v = fA[:, 0:W].rearrange("p (q a f i) -> p q a f i", a=2, f=2, i=64)
            uv = fB[:, 0:W].rearrange("p (q a f i) -> p q a f i", a=2, f=2, i=64)
            rv = fC[:, 0:W].rearrange("p (q a f i) -> p q a f i", a=2, f=2, i=64)
            for a in range(2):
                sa = tl if a == 0 else 16
                P.dve(lambda e, a=a, sa=sa: e.tensor_tensor(
                    out=t1v[:, :, a], in0=sv[:, :, a],
                    in1=RC[:, sa, :].unsqueeze(1).unsqueeze(1).to_broadcast([128, nq, 2, 64]), op=ALU.mult),
                    reads=[bsrc, brt], writes=[bfA])
                for f in range(2):
                    P.dve(lambda e, a=a, f=f, sa=sa: e.tensor_tensor(
                        out=uv[:, :, a, f], in0=sv[:, :, a, 1 - f],
                        in1=RS[:, sa, :].unsqueeze(1).to_broadcast([128, nq, 64]), op=ALU.mult),
                        reads=[bsrc, brt], writes=[bfB])
            P.pool(lambda e: e.tensor_tensor(out=rv[:, :, :, 0], in0=t1v[:, :, :, 0], in1=uv[:, :, :, 0], op=ALU.subtract),
                   reads=[bfA, bfB], writes=[bfC])
            P.pool(lambda e: e.tensor_tensor(out=rv[:, :, :, 1], in0=t1v[:, :, :, 1], in1=uv[:, :, :, 1], op=ALU.add),
                   reads=[bfA, bfB], writes=[bfC])

        def state_update(h, d, sl):
            for dc in range(2):
                ps, bps = bank()
                P.pe(lambda e, dc=dc, ps=ps: e.matmul(ps[:], lhsT=kdec[sl][:, dc * 128:(dc + 1) * 128], rhs=vbf[sl][:],
                                                      start=True, stop=True), reads=[bkdec[sl], bvbf[sl]], writes=[bps])
                P.dve(lambda e, dc=dc, ps=ps: e.scalar_tensor_tensor(
                    out=Sm[:, dc, :], in0=Sm[:, dc, :], scalar=rtab[:, 8, 16 + d * 4 + h:17 + d * 4 + h], in1=ps[:],
                    op0=ALU.mult, op1=ALU.add), reads=[bps, bSm[dc], bret], writes=[bSm[dc]])

        def head_pass(h):
            P.dma("pool", Wkv[:], rw_d[h][:, :, 0:768], writes=[bWkv])
            P.dma("pool", Wqg[:], rw_d[h][:, :, 768:1536], writes=[bWqg])
            P.dma("pool", Wo[:], rwo_d[h], writes=[bWo])
            P.dma("sp", gnb[:], rgn_d[0:1, h * 512:(h + 1) * 512].partition_broadcast(128), writes=[bgnb])
            P.phase(f'b{b}_ret_h{h}_B')
            P.dve(lambda e: e.memset(Sm[:], 0.0), writes=bSm)
            orderB = [1, 0] + list(range(17, 1, -1))

            def stageA_B(i, t):
                sl = i % 2
                pk, bpk = bank()
                pv, bpv = bank()
                for k in range(8):
                    P.pe(lambda e, k=k, pk=pk: e.matmul(pk[:, 0:256], lhsT=HT[:, k, t * 128:(t + 1) * 128], rhs=Wkv[:, k, 0:256],
                                                        start=(k == 0), stop=(k == 7)), reads=[bHT[t], bWkv], writes=[bpk])
                for k in range(8):
                    P.pe(lambda e, k=k, pv=pv: e.matmul(pv[:], lhsT=HT[:, k, t * 128:(t + 1) * 128], rhs=Wkv[:, k, 256:768],
                                                        start=(k == 0), stop=(k == 7)), reads=[bHT[t], bWkv], writes=[bpv])
                P.act(lambda e: e.activation(out=vbf[sl][:], in_=pv[:], func=AF.Copy), reads=[bpv], writes=[bvbf[sl]])
                if t >= 2:
                    rope(pk[:, 0:256], bpk, 1, t - 2)
                    P.dve(lambda e: e.tensor_scalar(out=kdec[sl][:], in0=fC[:, 0:256], scalar1=rtab[:, 8, 12 + h:13 + h],
                                                    scalar2=None, op0=ALU.mult), reads=[bfC, bret], writes=[bkdec[sl]])
                else:
                    P.dve(lambda e: e.tensor_scalar(out=kdec[sl][:], in0=pk[:, 0:256], scalar1=rtab[:, 8, 12 + h:13 + h],
                                                    scalar2=None, op0=ALU.mult), reads=[bpk, bret], writes=[bkdec[sl]])

            def stageB_B(i, t):
                if t >= 2:
                    P.act(lambda e: e.activation(out=Sst[:], in_=Sm[:], func=AF.Copy), reads=bSm, writes=[byy])
                    P.dma("sp", sb_d[t - 2], Sst[:].rearrange("p a b -> p (a b)"), reads=[byy], writes=[bsbd[t - 2]])
                state_update(h, 1, i % 2)

            for i in range(len(orderB) + 1):
                if i < len(orderB):
                    stageA_B(i, orderB[i])
                if i >= 1:
                    stageB_B(i - 1, orderB[i - 1])
            P.phase(f'b{b}_ret_h{h}_F')
            P.dve(lambda e: e.memset(Sm[:], 0.0), writes=bSm)
            P.pool(lambda e: e.memset(Sfb[:], 0.0), writes=bSfb)
            P.dma("sp", Sld[0][:].rearrange("p a b -> p (a b)"), sb_d[0], reads=[bsbd[0]], writes=[bSld[0]])
            ctxF = {}
            ctxA = {}
            ctxW = {}

            def stageA1_F(t):
                sl = t % 2
                lat = t >= 2
                pqk, bpqk = bank()
                pv, bpv = bank()
                if lat:
                    for k in range(8):
                        P.pe(lambda e, k=k: e.matmul(pqk[:, 0:256], lhsT=HT[:, k, t * 128:(t + 1) * 128], rhs=Wqg[:, k, 0:256],
                                                     start=(k == 0), stop=(k == 7)), reads=[bHT[t], bWqg], writes=[bpqk])
                for k in range(8):
                    P.pe(lambda e, k=k: e.matmul(pqk[:, 256:512], lhsT=HT[:, k, t * 128:(t + 1) * 128], rhs=Wkv[:, k, 0:256],
                                                 start=(k == 0), stop=(k == 7)), reads=[bHT[t], bWkv], writes=[bpqk])
                for k in range(8):
                    P.pe(lambda e, k=k: e.matmul(pv[:], lhsT=HT[:, k, t * 128:(t + 1) * 128], rhs=Wkv[:, k, 256:768],
                                                 start=(k == 0), stop=(k == 7)), reads=[bHT[t], bWkv], writes=[bpv])
                P.act(lambda e: e.activation(out=vbf[sl][:], in_=pv[:], func=AF.Copy), reads=[bpv], writes=[bvbf[sl]])
                if lat:
                    pg, bpg = bank()
                    for k in range(8):
                        P.pe(lambda e, k=k: e.matmul(pg[:], lhsT=HT[:, k, t * 128:(t + 1) * 128], rhs=Wqg[:, k, 256:768],
                                                     start=(k == 0), stop=(k == 7)), reads=[bHT[t], bWqg], writes=[bpg])
                    P.act(lambda e: e.activation(out=fD[sl][:], in_=pg[:], func=AF.Silu), reads=[bpg], writes=[bfD[sl]])
                ctxA[t] = (pqk, bpqk)

            def stageA1b_F(t):
                sl = t % 2
                lat = t >= 2
                pqk, bpqk = ctxA.pop(t)
                if lat:
                    rope(pqk[:], bpqk, 2, t - 2)
                    P.act(lambda e: e.activation(out=qkb[:, 0:512], in_=fC[:], func=AF.Copy), reads=[bfC], writes=[bqkb])
                    P.dve(lambda e: e.tensor_scalar(out=kdec[sl][:], in0=fC[:, 256:512], scalar1=rtab[:, 8, 8 + h:9 + h],
                                                    scalar2=None, op0=ALU.mult), reads=[bfC, bret], writes=[bkdec[sl]])
                    for d in range(2):
                        P.dve(lambda e, d=d: e.tensor_scalar(
                            out=qkb[:, 512 + 256 * d:768 + 256 * d], in0=fC[:, 0:256],
                            scalar1=rtab[:, 8, 24 + 4 * d + h:25 + 4 * d + h], scalar2=None, op0=ALU.mult),
                            reads=[bfC, bret, bqkb], writes=[bqkb])
                else:
                    P.dve(lambda e: e.tensor_scalar(out=kdec[sl][:], in0=pqk[:, 256:512], scalar1=rtab[:, 8, 8 + h:9 + h],
                                                    scalar2=None, op0=ALU.mult), reads=[bpqk, bret], writes=[bkdec[sl]])

            def stageA2_F(t):
                if t < 2:
                    return
                sl = t % 2
                if t + 1 < NT:
                    s2 = (t + 1) % 2
                    P.dma("sp", Sld[s2][:].rearrange("p a b -> p (a b)"), sb_d[t - 1], reads=[bsbd[t - 1]], writes=[bSld[s2]])
                pT, bpT = bank()
                pTb = pT.bitcast(BF16)
                for c in range(8):
                    P.pe(lambda e, c=c: e.transpose(pTb[:, c * 128:(c + 1) * 128], qkb[:, c * 128:(c + 1) * 128], identb[:]),
                         reads=[bqkb, bconst], writes=[bpT])
                P.act(lambda e: e.activation(out=QKT[sl][:, 0:4, :].rearrange("p a b -> p (a b)"), in_=pTb[:, 0:512], func=AF.Copy),
                      reads=[bpT], writes=[bQKT[sl]])
                P.dve(lambda e: e.tensor_copy(out=QKT[sl][:, 4:8, :].rearrange("p a b -> p (a b)"), in_=pTb[:, 512:1024]),
                      reads=[bpT, bQKT[sl]], writes=[bQKT[sl]])

            def stageB1_F(t):
                sl = t % 2
                if t >= 2:
                    pss, bpss = bank()
                    for dc in range(2):
                        P.pe(lambda e, dc=dc: e.matmul(pss[:, 0:128], lhsT=QKT[sl][:, 2 + dc, :], rhs=QKT[sl][:, dc, :],
                                                       start=(dc == 0), stop=(dc == 1)), reads=[bQKT[sl]], writes=[bpss])
                    P.dve(lambda e: e.tensor_tensor(out=sD[:], in0=pss[:, 0:128], in1=DT[:, h, :], op=ALU.mult),
                          reads=[bpss, bret], writes=[bsD])
                    po, bpo = bank()
                    ctxF[t] = (po, bpo)
                    P.pe(lambda e: e.matmul(po[:], lhsT=sD[:], rhs=vbf[sl][:], start=True, stop=False),
                         reads=[bsD, bvbf[sl]], writes=[bpo])
                    for dc in range(2):
                        P.pe(lambda e, dc=dc: e.matmul(po[:], lhsT=QKT[sl][:, 4 + dc, :], rhs=Sfb[:, dc, :], start=False, stop=False),
                             reads=[bQKT[sl], bSfb[dc]], writes=[bpo])
                    for dc in range(2):
                        P.pe(lambda e, dc=dc: e.matmul(po[:], lhsT=QKT[sl][:, 6 + dc, :], rhs=Sld[sl][:, dc, :],
                                                       start=False, stop=(dc == 1)),
                             reads=[bQKT[sl], bSld[sl]], writes=[bpo])
                state_update(h, 0, sl)
                if t + 1 < NT:
                    for dc in range(2):
                        P.act(lambda e, dc=dc: e.activation(out=Sfb[:, dc, :], in_=Sm[:, dc, :], func=AF.Copy),
                              reads=[bSm[dc]], writes=[bSfb[dc]])

            def stageB2_F(t):
                if t < 2:
                    return
                sl = t % 2
                po, bpo = ctxF.pop(t)
                P.dve(lambda e: e.bn_stats(out=st[:, 0:6], in_=po[:]), reads=[bpo], writes=[bst])
                P.dve(lambda e: e.bn_aggr(out=st[:, 6:8], in_=st[:, 0:6]), reads=[bst], writes=[bst])
                P.act(lambda e: e.activation(out=st[:, 8:9], in_=st[:, 7:8], func=AF.Ln, scale=1.0, bias=epsb[:, 0:1]),
                      reads=[bst, bconst], writes=[bst])
                P.act(lambda e: e.activation(out=st[:, 9:10], in_=st[:, 8:9], func=AF.Exp, scale=-0.5), reads=[bst], writes=[bst])
                P.dve(lambda e: e.scalar_tensor_tensor(out=st[:, 10:11], in0=st[:, 6:7], scalar=-1.0, in1=st[:, 9:10],
                                                       op0=ALU.mult, op1=ALU.mult), reads=[bst], writes=[bst])
                P.act(lambda e: e.activation(out=yn[:], in_=po[:], func=AF.Identity, scale=st[:, 9:10], bias=st[:, 10:11]),
                      reads=[bpo, bst], writes=[byn])
                P.pool(lambda e: e.tensor_tensor(out=yn[:], in0=yn[:], in1=gnb[:], op=ALU.mult), reads=[byn, bgnb], writes=[byn])
                P.dve(lambda e: e.tensor_tensor(out=ybf, in0=yn[:], in1=fD[sl][:], op=ALU.mult), reads=[byn, bfD[sl]], writes=[byy])

            def stageB2b_F(t):
                if t < 2:
                    return
                sl = t % 2
                pY, bpY = bank()
                pYb = pY.bitcast(BF16)
                for c in range(4):
                    P.pe(lambda e, c=c: e.transpose(pYb[:, c * 128:(c + 1) * 128], ybf[:, c * 128:(c + 1) * 128], identb[:]),
                         reads=[byy, bconst], writes=[bpY])
                P.act(lambda e: e.activation(out=yy[:, 1, :], in_=pYb[:, 0:512], func=AF.Copy), reads=[bpY], writes=[byy])
                for n in range(2):
                    pw, bpw = bank()
                    for c in range(4):
                        P.pe(lambda e, c=c, n=n, pw=pw: e.matmul(pw[:], lhsT=yT[:, c, :], rhs=Wo[:, c, n * 512:(n + 1) * 512],
                                                                 start=(c == 0), stop=(c == 3)), reads=[byy, bWo], writes=[bpw])
                    ctxW.setdefault(t, []).append((pw, bpw, n))

            def stageB2c_F(t):
                for (pw, bpw, n) in ctxW.pop(t, []):
                    x_update(pw, bpw, t, n, 0, tmps, btmps, upd[0])
                    upd[0] += 1

            for i in range(NT + 1):
                if i >= 1:
                    stageB1_F(i - 1)
                    stageB2_F(i - 1)
                if i < NT:
                    stageA1_F(i)
                if i >= 1:
                    stageB2b_F(i - 1)
                if i < NT:
                    stageA1b_F(i)
                if i >= 1:
                    stageB2c_F(i - 1)
                if i < NT:
                    stageA2_F(i)


        for h_ in range(4):
            head_pass(h_)

    def ffn(l, b, tiles):
        cv = Carver()
        has_ctx = tiles[0] < 2
        if has_ctx:
            Gbc[1] = cv.take([128, D], F32)
        gate_bcast(l, 1, b, 0)
        if has_ctx:
            gate_bcast(l, 1, NB, 1)
        Wout = cv.take([128, NJ, D], BF16)
        bWout = Buf("Wout")
        Win = [cv.take([128, 2, 8, 128], BF16) for _ in range(3)]
        bWin = [Buf(), Buf(), Buf()]
        GT = 6
        hidA = cv.take([128, 6, GT * 128], BF16)
        bhid = [Buf() for _ in range(NJ)]
        sg = [cv.take([128, 512], F32) for _ in range(2)]
        bsg = [Buf(), Buf()]
        tmps = sg; btmps = bsg
        ncv = Carver()
        ncv.off = cv.off

        def hid_ap(j, c0, c1):
            if j < 6:
                return hidA[:, j, c0:c1]
            jj = j - 6
            base = GT * 128 + (jj % 2) * GT * 128
            return HT[:, jj // 2, base + c0:base + c1]

        for q in range(11):
            P.dma("pool", Wout[:, q * 2:(q + 1) * 2, :], wout_d[l][:, q * 2:(q + 1) * 2, :], writes=[bWout])
        ng = (len(tiles) + GT - 1) // GT
        base_, extra = divmod(len(tiles), ng)
        groups = []
        i0 = 0
        for gi_ in range(ng):
            n_ = base_ + (1 if gi_ < extra else 0)
            groups.append(tiles[i0:i0 + n_])
            i0 += n_
        wi = 0
        k2 = 0
        upd = 0
        for ts in groups:
            nt_ = len(ts)
            ncv.off = cv.off
            norm_phase(l, 1, b, ts, ncv, slots=list(range(nt_)), nbuf=1)
            P.barrier()
            nb_ = (nt_ + 2) // 3
            bb, be = divmod(nt_, nb_)
            blocks = []
            c = 0
            for bi in range(nb_):
                n_ = bb + (1 if bi < be else 0)
                blocks.append((c, n_))
                c += n_
            for j in range(NJ):
                s = wi % 3
                wi += 1
                P.dma("pool", Win[s][:], win_d[l, j], writes=[bWin[s]])
                for (c0t, nbt) in blocks:
                    tok0 = c0t * 128
                    ntok = nbt * 128
                    pg, bpg = bank()
                    pu, bpu = bank()
                    for gu, (pp, bpp) in enumerate(((pg, bpg), (pu, bpu))):
                        for k in range(8):
                            P.pe(lambda e, k=k, gu=gu, pp=pp, s=s, ntok=ntok, tok0=tok0: e.matmul(
                                pp[:, 0:ntok], lhsT=Win[s][:, gu, k, :],
                                rhs=HT[:, k, tok0:tok0 + ntok], start=(k == 0), stop=(k == 7)),
                                reads=[bWin[s]] + [bHT[c0t + i] for i in range(nbt)], writes=[bpp])
                    q = k2 % 2
                    k2 += 1
                    P.act(lambda e, q=q, pg=pg, ntok=ntok: e.activation(out=sg[q][:, 0:ntok], in_=pg[:, 0:ntok], func=AF.Silu),
                          reads=[bpg], writes=[bsg[q]])
                    P.dve(lambda e, q=q, pu=pu, j=j, ntok=ntok, tok0=tok0: e.tensor_tensor(
                        out=hid_ap(j, tok0, tok0 + ntok), in0=sg[q][:, 0:ntok], in1=pu[:, 0:ntok], op=ALU.mult),
                        reads=[bsg[q], bpu], writes=[bhid[j]])
            for ti, t in enumerate(ts):
                gi = 1 if t < 2 else 0
                for n in range(2):
                    pw, bpw = bank()
                    for j in range(NJ):
                        P.pe(lambda e, j=j, n=n, pw=pw, ti=ti: e.matmul(
                            pw[:], lhsT=hid_ap(j, ti * 128, (ti + 1) * 128), rhs=Wout[:, j, n * 512:(n + 1) * 512],
                            start=(j == 0), stop=(j == NJ - 1)),
                            reads=[bhid[j], bWout], writes=[bpw])
                    x_update(pw, bpw, t, n, gi, tmps, btmps, upd)
                    upd += 1

    def finish():
        P.barrier()
        P.emit()
        return nc, P

    P.barrier()
    adaln()
    P.barrier()
    if stage == 1:
        P.dma("sp", dbg_d[:, 0:48 * R], modT[:, 0].rearrange("p j r -> p (j r)"), reads=[bmod])
        return finish()
    for b in range(NB):
        for t in range(2):
            P.dma("sp", X[:, t, :], ctx_d[b, t * 128:(t + 1) * 128, :], writes=[bX[t]])
        for t in range(2, NT):
            P.dma("sp", X[:, t, :], x_d[b, (t - 2) * 128:(t - 1) * 128, :], writes=[bX[t]])
        tiles = list(range(NT))
        P.phase(f'b{b}_attn')
        if stage == 2:
            cv = Carver()
            norm_phase(0, 0, b, [0, 1, 2, 3], cv)
            gate_bcast(0, 0, b, 0)
            P.barrier()
            tmpd = cv.take([128, 1024], F32)
            P.dve(lambda e: e.tensor_copy(out=tmpd[:, 0:512], in_=HT[:, 0:4, 0:128]))
            P.dve(lambda e: e.tensor_copy(out=tmpd[:, 512:1024], in_=HT[:, 0:4, 384:512]))
            P.barrier()
            P.dma("sp", dbg_d[:, 0:1024], tmpd[:])
            P.dma("sp", dbg_d[:, 1024:2048], Gbc[0][:])
            return finish()
        if stage == 5:
            retention(1, b)
            P.barrier()
            for t in range(2, NT):
                P.dma("sp", y_d[b, (t - 2) * 128:(t - 1) * 128, :], X[:, t, :], reads=[bX[t]])
            return finish()
        if stage != 4:
            attention(0, b, tiles)
        P.barrier()
        if stage == 3:
            for t in range(2, NT):
                P.dma("sp", y_d[b, (t - 2) * 128:(t - 1) * 128, :], X[:, t, :], reads=[bX[t]])
            return finish()
        P.phase(f'b{b}_ffn0')
        ffn(0, b, tiles)
        P.barrier()
        if n_layers > 1:
            P.phase(f'b{b}_ret')
            retention(1, b)
            P.barrier()
            P.phase(f'b{b}_ffn1')
            ffn(1, b, list(range(2, NT)))
            P.barrier()
        P.phase(f'b{b}_out')
        for t in range(2, NT):
            P.dma("sp", y_d[b, (t - 2) * 128:(t - 1) * 128, :], X[:, t, :], reads=[bX[t]])
    P.barrier()
    P.emit()
    return nc, P


def _c(a):
    return np.ascontiguousarray(a, dtype=np.float32)


def prep_shared(inp):
    sh = {}
    ada_w = np.asarray(inp["ada_w"], np.float32)
    sh["ada_w_r"] = _c(ada_w.reshape(2, 8, 128, 12, 512).transpose(0, 3, 2, 1, 4))
    sh["ada_bT"] = _c(np.asarray(inp["ada_b"]).reshape(2, 48, 128).transpose(0, 2, 1))
    sh["norm1_gT"] = _c(np.asarray(inp["norm1_g"]).reshape(2, 8, 128).transpose(0, 2, 1))
    sh["norm2_gT"] = _c(np.asarray(inp["norm2_g"]).reshape(2, 8, 128).transpose(0, 2, 1))
    w_in = np.asarray(inp["ffn_w_in"], np.float32).reshape(2, 8, 128, 2, NJ, 128)
    sh["ffn_w_in_r"] = _c(w_in.transpose(0, 4, 2, 3, 1, 5))
    w_out = np.asarray(inp["ffn_w_out"], np.float32).reshape(2, NJ, 128, D)
    sh["ffn_w_out_r"] = _c(w_out.transpose(0, 2, 1, 3))
    wqkv = np.asarray(inp["attn_w_qkv"], np.float32)[0]
    parts = []
    for g in range(4):
        blk = np.concatenate([wqkv[:, g * 256:(g + 1) * 256], wqkv[:, 1024 + g * 64:1024 + (g + 1) * 64],
                              wqkv[:, 1280 + g * 64:1280 + (g + 1) * 64]], axis=1)
        parts.append(blk.reshape(8, 128, 384).transpose(1, 0, 2))
    sh["attn_w_qkv_r"] = _c(np.stack(parts))
    awo = np.asarray(inp["attn_w_o"], np.float32)[0]
    sh["attn_w_o_r"] = _c(awo.reshape(4, 4, 64, D).transpose(0, 2, 1, 3))
    gq = np.asarray(inp["attn_q_norm"], np.float32)[0]
    gk = np.asarray(inp["attn_k_norm"], np.float32)[0]
    sh["attn_gains"] = _c(np.concatenate([gq, gq, gq, gq, gk])[None, :])
    sh["attn_sink"] = _c(np.asarray(inp["attn_sink"]).reshape(1, 16))
    rw = np.asarray(inp["ret_w_qkvg"], np.float32)[0]
    parts = []
    for h in range(4):
        blk = np.concatenate([rw[:, 1024 + h * 256:1024 + (h + 1) * 256], rw[:, 2048 + h * 512:2048 + (h + 1) * 512],
                              rw[:, h * 256:(h + 1) * 256], rw[:, 4096 + h * 512:4096 + (h + 1) * 512]], axis=1)
        parts.append(blk.reshape(8, 128, 1536).transpose(1, 0, 2))
    sh["ret_w_r"] = _c(np.stack(parts))
    sh["ret_decay_logit"] = _c(np.asarray(inp["ret_decay_logit"]).reshape(1, 8))
    sh["ret_gn_g"] = _c(np.asarray(inp["ret_gn_g"]).reshape(1, 2048))
    sh["ret_w_o_r"] = _c(np.asarray(inp["ret_w_o"], np.float32)[0].reshape(4, 4, 128, D).transpose(0, 2, 1, 3))
    sh.update(make_consts())
    return sh


def core_inputs(inp, sh, b0, NB):
    m = dict(sh)
    m["x"] = _c(np.asarray(inp["x"])[b0:b0 + NB])
    m["ctx"] = _c(np.asarray(inp["ctx"])[b0:b0 + NB])
    cc = np.concatenate([np.asarray(inp["c"], np.float32)[b0:b0 + NB], np.asarray(inp["c_ctx"], np.float32)[None, :]], axis=0)
    m["cT"] = _c(cc.reshape(NB + 1, 8, 128).transpose(2, 1, 0))
    return m


def kernel(**inp):
    NB = 4
    nc, _ = build_program(NB)
    sh = prep_shared(inp)
    in_maps = [core_inputs(inp, sh, c * NB, NB) for c in range(8)]
    res = run_bass_kernel_spmd(nc, in_maps, core_ids=list(range(8)))
    return np.concatenate([r["y"] for r in res.results], axis=0).astype(np.float32)
```
